# Optimizing a Trainium2 kernel written in Bass

```python
import jax, jax.numpy as jnp
from jax import lax
import numpy as np

D_MODEL = 1024
BATCH = 8
SEQ = 2048
DEPTH = 2

EPS = 1e-6
N_EVEN = (DEPTH + 1) // 2
N_ODD = DEPTH // 2

A_HEADS = 8
A_HEAD_DIM = 64
A_WIDTH = A_HEADS * A_HEAD_DIM
CONV_WIDTH = 3
POOL_WINDOWS = (2, 4, 8, 16)
B_GROUPS = len(POOL_WINDOWS)
B_GROUP_DIM = 128
B_WIDTH = B_GROUPS * B_GROUP_DIM
AB_IN = 3 * A_WIDTH + B_WIDTH
AB_OUT = A_WIDTH + B_WIDTH

C_HEADS = 8
C_NOPE = 64
C_ROPE = 32
C_V = 64
C_Q_RANK = 256
C_KV_RANK = 128
C_WIDTH = C_HEADS * C_V
ROPE_THETA = 10000.0
ATTN_BLOCK = 128
D_GROUPS = 4
D_GROUP_DIM = 128
D_WIDTH = D_GROUPS * D_GROUP_DIM
D_CHUNK = 128
CD_IN = C_Q_RANK + C_KV_RANK + C_ROPE + 2 * D_WIDTH
CD_OUT = C_WIDTH + D_WIDTH

D_FF = 2816
N_MOD = 6

kernel_name = "hybrid_conv_pool_mla_gmlp_block"


def rms_norm(x, g):
    xf = x.astype(jnp.float32)
    y = xf * lax.rsqrt(jnp.mean(xf * xf, axis=-1, keepdims=True) + EPS)
    return (y * g.astype(jnp.float32)).astype(x.dtype)


def layer_norm(x, g, b):
    xf = x.astype(jnp.float32)
    mu = jnp.mean(xf, axis=-1, keepdims=True)
    xc = xf - mu
    y = xc * lax.rsqrt(jnp.mean(xc * xc, axis=-1, keepdims=True) + EPS)
    return (y * g.astype(jnp.float32) + b.astype(jnp.float32)).astype(x.dtype)


def causal_dwconv(x, w):
    K, C = w.shape
    return lax.conv_general_dilated(
        x, w[:, None, :].astype(x.dtype), window_strides=(1,),
        padding=((K - 1, 0),), dimension_numbers=("NWC", "WIO", "NWC"),
        feature_group_count=C)


def rope_tables(positions):
    half = C_ROPE // 2
    inv_freq = ROPE_THETA ** (-jnp.arange(half, dtype=jnp.float32) / half)
    ang = positions.astype(jnp.float32)[..., None] * inv_freq
    return jnp.cos(ang), jnp.sin(ang)


def apply_rope(x, cos, sin):
    if x.ndim == 4:
        cos, sin = cos[:, :, None, :], sin[:, :, None, :]
    xf = x.astype(jnp.float32)
    x1, x2 = jnp.split(xf, 2, axis=-1)
    return jnp.concatenate([x1 * cos - x2 * sin, x2 * cos + x1 * sin], axis=-1).astype(x.dtype)


def short_gated_conv(b_gate, c_gate, h, conv_w):
    return b_gate * causal_dwconv(c_gate * h, conv_w)


def multiscale_pool(p, mix_w, scale):
    Bn, S, _ = p.shape
    pg = p.reshape(Bn, S, B_GROUPS, B_GROUP_DIM)
    cs = jnp.cumsum(pg.astype(jnp.float32), axis=1)
    t = jnp.arange(S)
    pooled = []
    for g, w in enumerate(POOL_WINDOWS):
        csg = cs[:, :, g]
        lag = jnp.pad(csg[:, :S - w], ((0, 0), (w, 0), (0, 0)))
        cnt = jnp.minimum(t + 1, w).astype(jnp.float32)[None, :, None]
        pooled.append((csg - lag) / cnt)
    pooled = jnp.stack(pooled, axis=2).astype(p.dtype) - pg
    y = jnp.einsum("bsgc,gcd->bsgd", pooled, mix_w)
    return y.reshape(Bn, S, B_WIDTH) * scale


def latent_attention(q_lat, kv_lat, k_pe, q_norm_g, w_uq, kv_norm_g, w_ukv, cos, sin):
    Bn, S, _ = q_lat.shape
    q = (rms_norm(q_lat, q_norm_g) @ w_uq).reshape(Bn, S, C_HEADS, C_NOPE + C_ROPE)
    q_nope, q_pe = q[..., :C_NOPE], apply_rope(q[..., C_NOPE:], cos, sin)
    kv = (rms_norm(kv_lat, kv_norm_g) @ w_ukv).reshape(Bn, S, C_HEADS, C_NOPE + C_V)
    k_nope, v = kv[..., :C_NOPE], kv[..., C_NOPE:]
    k_pe = apply_rope(k_pe, cos, sin)
    scale = (C_NOPE + C_ROPE) ** -0.5
    nb = S // ATTN_BLOCK
    qn_blocks = q_nope.reshape(Bn, nb, ATTN_BLOCK, C_HEADS, C_NOPE).transpose(1, 0, 2, 3, 4)
    qp_blocks = q_pe.reshape(Bn, nb, ATTN_BLOCK, C_HEADS, C_ROPE).transpose(1, 0, 2, 3, 4)
    key_pos = jnp.arange(S)

    def one_block(args):
        i, qn, qp = args
        s = (jnp.einsum("bqhd,bkhd->bhqk", qn, k_nope, preferred_element_type=jnp.float32)
             + jnp.einsum("bqhr,bkr->bhqk", qp, k_pe, preferred_element_type=jnp.float32)) * scale
        q_pos = i * ATTN_BLOCK + jnp.arange(ATTN_BLOCK)
        s = jnp.where(key_pos[None, :] <= q_pos[:, None], s, -jnp.inf)
        prob = jax.nn.softmax(s, axis=-1).astype(v.dtype)
        return jnp.einsum("bhqk,bkhd->bqhd", prob, v)

    out = lax.map(one_block, (jnp.arange(nb), qn_blocks, qp_blocks))
    return out.transpose(1, 0, 2, 3, 4).reshape(Bn, S, C_WIDTH)


def spatial_gating(u, v, ln_g, ln_b, w_s, b_s):
    Bn, S, _ = v.shape
    v = layer_norm(v, ln_g, ln_b)
    vc = v.reshape(Bn, S // D_CHUNK, D_CHUNK, D_GROUPS, D_GROUP_DIM)
    mask = jnp.tril(jnp.ones((D_CHUNK, D_CHUNK), dtype=bool))
    w = jnp.where(mask[None], w_s, 0)
    mixed = jnp.einsum("gts,bnsgc->bntgc", w, vc) + b_s.T[None, None, :, :, None]
    return u * mixed.reshape(Bn, S, D_WIDTH)


def conv_ffn(h, w_up, conv_w, w_down):
    z = causal_dwconv(h @ w_up, conv_w)
    g, u = jnp.split(z, 2, axis=-1)
    return (jax.nn.silu(g) * u) @ w_down


def setup_inputs(seed: int = 0) -> dict:
    key = jax.random.key(seed)
    ks = iter(jax.random.split(key, 32))
    nrm = lambda shape, s: jax.random.normal(next(ks), shape, jnp.float32) * s
    gain = lambda shape: 1.0 + nrm(shape, 0.05)
    D = D_MODEL
    offsets = jax.random.randint(next(ks), (BATCH, 1), 0, 4096, dtype=jnp.int32)
    positions = (jnp.arange(SEQ, dtype=jnp.int32)[None, :] + offsets).astype(jnp.int32)
    return {
        "x": nrm((BATCH, SEQ, D), 1.0),
        "c": nrm((BATCH, D), 1.0),
        "positions": positions,
        "ada_w": nrm((DEPTH, D, N_MOD * D), 0.5 * D ** -0.5),
        "ada_b": nrm((DEPTH, N_MOD * D), 0.02),
        "norm1_g": gain((DEPTH, D)),
        "norm2_g": gain((DEPTH, D)),
        "ab_w_in": nrm((N_EVEN, D, AB_IN), D ** -0.5),
        "a_conv_w": nrm((N_EVEN, CONV_WIDTH, A_WIDTH), CONV_WIDTH ** -0.5),
        "b_mix_w": nrm((N_EVEN, B_GROUPS, B_GROUP_DIM, B_GROUP_DIM), B_GROUP_DIM ** -0.5),
        "b_scale": 1.0 + nrm((N_EVEN, B_WIDTH), 0.1),
        "ab_w_out": nrm((N_EVEN, AB_OUT, D), AB_OUT ** -0.5),
        "cd_w_in": nrm((N_ODD, D, CD_IN), D ** -0.5),
        "c_q_norm_g": gain((N_ODD, C_Q_RANK)),
        "c_w_uq": nrm((N_ODD, C_Q_RANK, C_HEADS * (C_NOPE + C_ROPE)), C_Q_RANK ** -0.5),
        "c_kv_norm_g": gain((N_ODD, C_KV_RANK)),
        "c_w_ukv": nrm((N_ODD, C_KV_RANK, C_HEADS * (C_NOPE + C_V)), C_KV_RANK ** -0.5),
        "d_ln_g": gain((N_ODD, D_WIDTH)),
        "d_ln_b": nrm((N_ODD, D_WIDTH), 0.02),
        "d_w_s": nrm((N_ODD, D_GROUPS, D_CHUNK, D_CHUNK), 0.5 * D_CHUNK ** -0.5),
        "d_b_s": 1.0 + nrm((N_ODD, D_GROUPS, D_CHUNK), 0.02),
        "cd_w_out": nrm((N_ODD, CD_OUT, D), CD_OUT ** -0.5),
        "ffn_w_up": nrm((DEPTH, D, 2 * D_FF), D ** -0.5),
        "ffn_conv_w": nrm((DEPTH, CONV_WIDTH, 2 * D_FF), CONV_WIDTH ** -0.5),
        "ffn_w_down": nrm((DEPTH, D_FF, D), D_FF ** -0.5),
        "final_norm_g": gain((D,)),
    }


def reference(x, c, positions, ada_w, ada_b, norm1_g, norm2_g,
              ab_w_in, a_conv_w, b_mix_w, b_scale, ab_w_out,
              cd_w_in, c_q_norm_g, c_w_uq, c_kv_norm_g, c_w_ukv,
              d_ln_g, d_ln_b, d_w_s, d_b_s, cd_w_out,
              ffn_w_up, ffn_conv_w, ffn_w_down, final_norm_g):
    cos, sin = rope_tables(positions)
    c_act = jax.nn.silu(c)
    for l in range(DEPTH):
        mod = c_act @ ada_w[l] + ada_b[l]
        sh1, sc1, g1, sh2, sc2, g2 = [m[:, None, :] for m in jnp.split(mod, N_MOD, axis=-1)]
        h = rms_norm(x, norm1_g[l]) * (1 + sc1) + sh1
        i = l // 2
        if l % 2 == 0:
            z = h @ ab_w_in[i]
            b_gate, c_gate, a_in, p = jnp.split(z, [A_WIDTH, 2 * A_WIDTH, 3 * A_WIDTH], axis=-1)
            y_a = short_gated_conv(b_gate, c_gate, a_in, a_conv_w[i])
            y_b = multiscale_pool(p, b_mix_w[i], b_scale[i])
            y = jnp.concatenate([y_a, y_b], axis=-1) @ ab_w_out[i]
        else:
            z = h @ cd_w_in[i]
            q_lat, kv_lat, k_pe, uv = jnp.split(
                z, [C_Q_RANK, C_Q_RANK + C_KV_RANK, C_Q_RANK + C_KV_RANK + C_ROPE], axis=-1)
            y_c = latent_attention(q_lat, kv_lat, k_pe, c_q_norm_g[i], c_w_uq[i],
                                   c_kv_norm_g[i], c_w_ukv[i], cos, sin)
            u, v = jnp.split(jax.nn.gelu(uv), 2, axis=-1)
            y_d = spatial_gating(u, v, d_ln_g[i], d_ln_b[i], d_w_s[i], d_b_s[i])
            y = jnp.concatenate([y_c, y_d], axis=-1) @ cd_w_out[i]
        x = x + g1 * y
        h = rms_norm(x, norm2_g[l]) * (1 + sc2) + sh2
        x = x + g2 * conv_ffn(h, ffn_w_up[l], ffn_conv_w[l], ffn_w_down[l])
    return rms_norm(x, final_norm_g)
```

```python
import numpy as np
from contextlib import ExitStack
import concourse.bass as bass
import concourse.mybir as mybir
from concourse.bass_utils import run_bass_kernel_spmd

F32 = mybir.dt.float32
BF16 = mybir.dt.bfloat16
I32 = mybir.dt.int32
AF = mybir.ActivationFunctionType
ALU = mybir.AluOpType

S = 2048
D = 1024
NT = 4
TT = 512
DFF = 2816
NPAIR = 22
EPS = 1e-6
SLOT_ELEMS = 4096
NSLOT = 4
PREFETCH = 2
JG = 4

FFN_TILES = [(0, 512, 0, 512, 0), (510, 512, 512, 510, 2), (1020, 512, 1022, 510, 2),
             (1530, 512, 1532, 510, 2), (2040, 8, 2042, 6, 2)]


class Res:
    __slots__ = ("name", "w", "r", "dsem", "dcnt", "tile_id")

    def __init__(self, name):
        self.name = name
        self.w = {}
        self.r = {}
        self.dsem = None
        self.dcnt = 0
        self.tile_id = None


class Ctx:
    def __init__(self, nc, es):
        self.nc = nc
        self.es = es
        self.eng = dict(pe=nc.tensor, act=nc.scalar, dve=nc.vector, pool=nc.gpsimd, sp=nc.sync)
        self.sem = {k: es.enter_context(nc.semaphore("s_" + k)) for k in self.eng}
        self.cnt = {k: 0 for k in self.eng}
        self.waited = {k: {} for k in self.eng}
        self.nsem = 0
        self.uid = 0

    def name(self, base):
        self.uid += 1
        return "%s_%d" % (base, self.uid)

    def newsem(self, base="d"):
        self.nsem += 1
        return self.es.enter_context(self.nc.semaphore("%s%d" % (base, self.nsem)))

    def _deps(self, e, reads, writes, skip=None):
        need = {}
        own_pe = id(self.sem["pe"]) if e == "pe" else skip

        def add(d):
            for k, (s, v) in d.items():
                if k == own_pe:
                    continue
                if k not in need or need[k][1] < v:
                    need[k] = (s, v)

        for r in reads:
            add(r.w)
        for w in writes:
            add(w.w)
            add(w.r)
        wd = self.waited[e]
        for k, (s, v) in need.items():
            if wd.get(k, 0) < v:
                self.eng[e].wait_ge(s, v)
                wd[k] = v

    def _post(self, t, reads, writes):
        s, v = t
        k = id(s)
        for r in reads:
            if k not in r.r or r.r[k][1] < v:
                r.r[k] = t
        for w in writes:
            w.w[k] = t
            w.r = {}

    def op(self, e, fn, reads=(), writes=(), inc=True):
        self._deps(e, reads, writes)
        ins = fn()
        if inc:
            self.cnt[e] += 1
            ins.then_inc(self.sem[e], 1)
            t = (self.sem[e], self.cnt[e])
        else:
            t = (self.sem[e], self.cnt[e] + 1)
        self._post(t, reads, writes)
        return t

    def dma(self, q, out, in_, reads=(), writes=(), done=None):
        d = done if done is not None else writes[0]
        if d.dsem is None:
            d.dsem = self.newsem()
        self._deps(q, reads, writes, skip=id(d.dsem))
        d.dcnt += 16
        self.eng[q].dma_start(out=out, in_=in_).then_inc(d.dsem, 16)
        t = (d.dsem, d.dcnt)
        self._post(t, reads, writes)
        return t

    def sync_to(self, e, engs=("pe", "act", "dve")):
        for f in engs:
            k = id(self.sem[f])
            if self.waited[e].get(k, 0) < self.cnt[f]:
                self.eng[e].wait_ge(self.sem[f], self.cnt[f])
                self.waited[e][k] = self.cnt[f]

    def barrier(self, engs=("pe", "act", "dve")):
        for e in engs:
            for f in engs:
                if f == e:
                    continue
                k = id(self.sem[f])
                if self.waited[e].get(k, 0) < self.cnt[f]:
                    self.eng[e].wait_ge(self.sem[f], self.cnt[f])
                    self.waited[e][k] = self.cnt[f]


class Ring:
    def __init__(self, items):
        self.items = items
        self.i = 0

    def next(self):
        it = self.items[self.i % len(self.items)]
        self.i += 1
        return it


class WStream:
    def __init__(self, c, T, slots, slot_res, plan):
        self.c = c
        self.T = T
        self.slots = slots
        self.res = slot_res
        self.plan = plan
        self.rec = []
        self.k = 0
        self.issued = 0

    def _issue(self, idx, fn):
        s = idx % NSLOT
        res = self.res[s]
        pieces = fn(self.T, self.slots[s])
        for (o, i) in pieces:
            self.c.dma("pool", o, i, reads=(), writes=(res,))
        res.tile_id = idx

    def get(self, fn):
        k = self.k
        self.k += 1
        self.rec.append(fn)
        if self.plan is None:
            upto = k
            fns = self.rec
        else:
            upto = min(k + PREFETCH, len(self.plan) - 1)
            fns = self.plan
        while self.issued <= upto:
            self._issue(self.issued, fns[self.issued])
            self.issued += 1
        s = k % NSLOT
        return self.slots[s], self.res[s], k

    def check(self, res, k):
        assert res.tile_id == k, "weight slot reused before consumption (%s vs %s)" % (res.tile_id, k)


def wview(slot, nk, ncols):
    return slot[:, 0:nk * ncols].rearrange("p (k n) -> p k n", k=nk)


def build(plan=None, stop=None):
    nc = bass.Bass("TRN2", target_bir_lowering=False)
    es = ExitStack()
    c = Ctx(nc, es)
    stages = ["load", "l0mix", "l0ffn", "l1mix", "l1ffn"]
    upto = len(stages) if stop is None else stages.index(stop)

    def din(name, shape, dt=F32):
        return nc.dram_tensor(name, list(shape), dt, kind="ExternalInput").ap()

    T = {}
    T["x"] = din("x", [S, D])
    T["pos"] = din("pos", [128, S], I32)
    T["cols"] = din("cols", [128, NCOLS])
    T["lnbc"] = din("lnbc", [128, 2, 512])
    T["bs"] = din("bs", [1, 4, 128])
    T["fgbc"] = din("fgbc", [128, D])
    T["ada_w"] = din("ada_w", [2, D, 6 * D])
    T["ab_w_in"] = din("ab_w_in", [D, 2048])
    T["b_mix_w"] = din("b_mix_w", [4, 128, 128])
    T["ab_w_out"] = din("ab_w_out", [D, D])
    T["cd_w_in"] = din("cd_w_in", [D, 1440])
    T["c_w_uq"] = din("c_w_uq", [256, 768])
    T["c_w_ukv"] = din("c_w_ukv", [128, 1024])
    T["d_w_sT"] = din("d_w_sT", [4, 128, 128])
    T["cd_w_out"] = din("cd_w_out", [D, D])
    T["ffn_w_up"] = din("ffn_w_up", [2, D, 2 * DFF])
    T["ffn_w_down"] = din("ffn_w_down", [2, DFF, D])
    out_d = nc.dram_tensor("out", [S, D], F32, kind="ExternalOutput").ap()

    def sb(shape, dt, base="t", stack=None):
        return (stack or es).enter_context(nc.sbuf_tensor(c.name(base), list(shape), dt))

    xT = sb([128, 8, S], F32, "xT")
    x_res = [[Res("x%d_%d" % (ci, t)) for t in range(NT)] for ci in range(8)]
    cols = sb([128, NCOLS], F32, "cols")
    cols_res = Res("cols")
    modc = sb([128, 2, 48], F32, "modc")
    modA = sb([128, 2, 2, 8], F32, "modA")
    mod_res_l = [Res("mod0"), Res("mod1")]
    cact = sb([128, 8], BF16, "cact")
    cact_res = Res("cact")
    bg = []
    ident = sb([128, 128], F32, "ident")
    ones_bf = sb([128, 128], BF16, "ones")
    ones_f = sb([1, 128], F32, "onesf")
    setup_res = Res("setup")
    slots = [sb([128, SLOT_ELEMS], BF16, "slot") for _ in range(NSLOT)]
    slot_res = [Res("slot%d" % i) for i in range(NSLOT)]
    ps = es.enter_context(nc.psum_tensor("ps", [128, 8, 512], F32))
    bank_res = [Res("bank%d" % i) for i in range(8)]
    ws = WStream(c, T, slots, slot_res, plan)

    def bank(i):
        return ps[:, i, :]

    def colv(name, j=0, n=1):
        o = COLOFF[name] + j
        return cols[:, o:o + n]

    c.dma("sp", cols[:], T["cols"], writes=(cols_res,))
    c.op("pool", lambda: nc.gpsimd.memset(ident[:], 1.0), writes=(setup_res,))
    c.op("pool", lambda: nc.gpsimd.affine_select(out=ident[:], in_=ident[:], pattern=[[-1, 128]],
                                                  compare_op=ALU.is_equal, fill=0.0, base=0,
                                                  channel_multiplier=1),
         reads=(setup_res,), writes=(setup_res,))
    c.op("pool", lambda: nc.gpsimd.memset(ones_bf[:], 1.0), writes=(setup_res,))
    c.op("pool", lambda: nc.gpsimd.memset(ones_f[:], 1.0), writes=(setup_res,))
    tril = sb([128, 128], F32, "tril")
    c.op("pool", lambda: nc.gpsimd.memset(tril[:], 1.0), writes=(setup_res,))
    c.op("pool", lambda: nc.gpsimd.affine_select(out=tril[:], in_=tril[:], pattern=[[1, 128]],
                                                  compare_op=ALU.is_ge, fill=0.0, base=0,
                                                  channel_multiplier=-1),
         reads=(setup_res,), writes=(setup_res,))
    invc = sb([128, 4, 16], F32, "invc")
    invc_res = Res("invc")
    i16 = sb([128, 16], F32, "i16")
    c.op("pool", lambda: nc.gpsimd.iota(i16[:], pattern=[[1, 16]], base=1, channel_multiplier=0,
                                        allow_small_or_imprecise_dtypes=True), writes=(invc_res,))
    for g in range(4):
        c.op("dve", lambda g=g: nc.vector.tensor_scalar(out=invc[:, g, :], in0=i16[:], scalar1=float(2 ** (g + 1)),
                                                        scalar2=None, op0=ALU.min),
             reads=(invc_res,), writes=(invc_res,))
    c.op("dve", lambda: nc.vector.reciprocal(out=invc[:], in_=invc[:]), reads=(invc_res,), writes=(invc_res,))

    tab_d = nc.dram_tensor("ropetab", [2, 128, S], F32, kind="Internal").ap()
    tabw_res = Res("tabw")
    c.op("act", lambda: nc.scalar.activation(out=cact[:], in_=colv("c", 0, 8), func=AF.Silu),
         reads=(cols_res,), writes=(cact_res,))
    with ExitStack() as ph:
        stg = [sb([128, D], F32, "xstg", ph) for _ in range(3)]
        stg_res = [Res("xstg%d" % i) for i in range(3)]
        ring = Ring(list(range(8)))
        if upto >= 3:
            Ct = sb([128, S], F32, "Ct0", ph)
            Sg = sb([128, S], F32, "Sg0", ph)
            tab_res = Res("tab0")
            posi = sb([128, S], I32, "posi", ph)
            ang = sb([128, S], F32, "ang", ph)
            kq = sb([128, S], F32, "kq", ph)
            rr = sb([128, S], F32, "rr", ph)
            pr, ar, kr, rres = Res("posi"), Res("ang"), Res("kq"), Res("rr")
            c.dma("sp", posi[:], T["pos"], writes=(pr,))
            c.op("dve", lambda: nc.vector.tensor_copy(out=ang[:], in_=posi[:]), reads=(pr,), writes=(ar,))
            c.op("dve", lambda: nc.vector.tensor_scalar(out=ang[:], in0=ang[:], scalar1=colv("inv_freq"), scalar2=None,
                                                        op0=ALU.mult), reads=(ar, cols_res), writes=(ar,))
            MAGIC = 12582912.0
            C1 = 6.28125
            C2 = 2.0 * np.pi - 6.28125
            for which, dst in ((0, Sg), (1, Ct)):
                src = ang
                if which == 1:
                    c.op("dve", lambda: nc.vector.tensor_scalar(out=rr[:], in0=ang[:], scalar1=float(np.pi / 2), scalar2=None,
                                                                op0=ALU.add), reads=(ar,), writes=(rres,))
                    src = rr
                sres = ar if which == 0 else rres
                c.op("dve", lambda src=src: nc.vector.tensor_scalar(out=kq[:], in0=src[:], scalar1=float(1.0 / (2 * np.pi)),
                                                                    scalar2=MAGIC, op0=ALU.mult, op1=ALU.add),
                     reads=(sres,), writes=(kr,))
                c.op("dve", lambda: nc.vector.tensor_scalar(out=kq[:], in0=kq[:], scalar1=-MAGIC, scalar2=None, op0=ALU.add),
                     reads=(kr,), writes=(kr,))
                c.op("dve", lambda src=src: nc.vector.scalar_tensor_tensor(out=rr[:], in0=kq[:], scalar=-C1, in1=src[:],
                                                                           op0=ALU.mult, op1=ALU.add),
                     reads=(kr, sres, rres), writes=(rres,))
                c.op("dve", lambda: nc.vector.scalar_tensor_tensor(out=rr[:], in0=kq[:], scalar=float(-C2), in1=rr[:],
                                                                   op0=ALU.mult, op1=ALU.add),
                     reads=(kr, rres), writes=(rres,))
                c.op("dve", lambda: nc.vector.tensor_scalar(out=rr[:], in0=rr[:], scalar1=float(np.pi), scalar2=float(-np.pi),
                                                            op0=ALU.min, op1=ALU.max), reads=(rres,), writes=(rres,))
                c.op("act", lambda dst=dst: nc.scalar.activation(out=dst[:], in_=rr[:], func=AF.Sin),
                     reads=(rres,), writes=(tab_res,))
            c.op("dve", lambda: nc.vector.tensor_scalar(out=Sg[:], in0=Sg[:], scalar1=colv("rope_sign"), scalar2=None,
                                                        op0=ALU.mult), reads=(tab_res, cols_res), writes=(tab_res,))

        for tb in range(16):
            si = tb % 3
            c.dma("sp", stg[si][:], T["x"][tb * 128:(tb + 1) * 128, :], writes=(stg_res[si],))
            tt = tb // 4
            for half in range(2):
                b = ring.next()
                for q in range(4):
                    ci = half * 4 + q
                    c.op("pe", lambda ci=ci, b=b, q=q, si=si: nc.tensor.transpose(
                        ps[:, b, q * 128:(q + 1) * 128], stg[si][:, ci * 128:(ci + 1) * 128], ident[:]),
                        reads=(stg_res[si], setup_res), writes=(bank_res[b],), inc=(q == 3))
                eng = "act"
                dst = xT[:, half * 4:half * 4 + 4, tb * 128:(tb + 1) * 128]
                src = ps[:, b, :].rearrange("p (q n) -> p q n", q=4)
                wr = tuple(x_res[half * 4 + q][tt] for q in range(4))
                if eng == "act":
                    c.op("act", lambda dst=dst, src=src: nc.scalar.activation(out=dst, in_=src, func=AF.Copy),
                         reads=(bank_res[b],), writes=wr)
                else:
                    c.op("dve", lambda dst=dst, src=src: nc.vector.tensor_copy(out=dst, in_=src),
                         reads=(bank_res[b],), writes=wr)
        if upto >= 3:
            c.dma("sp", tab_d[0], Ct[:], reads=(tab_res,), writes=(), done=tabw_res)
            c.dma("sp", tab_d[1], Sg[:], reads=(tab_res,), writes=(), done=tabw_res)
            nc.tensor.wait_ge(tabw_res.dsem, tabw_res.dcnt)
            nc.vector.wait_ge(tabw_res.dsem, tabw_res.dcnt)
            nc.scalar.wait_ge(tabw_res.dsem, tabw_res.dcnt)
        c.barrier()


    def mod_tile(l, i, b):
        slot, sres, k = ws.get(lambda T, slot, i=i, l=l: [(
            wview(slot, 8, 512),
            T["ada_w"][l].rearrange("(k p) n -> p k n", p=128)[:, :, i * 512:(i + 1) * 512])])
        wv = wview(slot, 8, 512)
        for jj in range(4):
            for kk in range(8):
                ws.check(sres, k)
                c.op("pe", lambda jj=jj, kk=kk: nc.tensor.matmul(
                    ps[:, b, jj:jj + 1], lhsT=wv[:, kk, jj * 128:(jj + 1) * 128], rhs=cact[:, kk:kk + 1],
                    start=(kk == 0), stop=(kk == 7)),
                    reads=(sres, cact_res), writes=(bank_res[b],), inc=(kk == 7))
        c.op("dve", lambda: nc.vector.tensor_tensor(out=modc[:, l, i * 4:i * 4 + 4], in0=ps[:, b, 0:4],
                                                    in1=colv("ada_b", l * 48 + i * 4, 4), op=ALU.add),
             reads=(bank_res[b], cols_res), writes=(mod_res_l[l],))

    def mod_A(l, n):
        gname, so = (("norm1_g", 8), ("norm2_g", 32))[n]
        c.op("dve", lambda: nc.vector.scalar_tensor_tensor(
            out=modA[:, l, n, :], in0=modc[:, l, so:so + 8], scalar=1.0, in1=colv(gname, l * 8, 8),
            op0=ALU.add, op1=ALU.mult),
            reads=(mod_res_l[l], cols_res), writes=(mod_res_l[l],))

    def bg_step(ring):
        if bg:
            bg.pop(0)(ring.next())

    def bg_flush(ring):
        while bg:
            bg.pop(0)(ring.next())

    def emit_norm(h, h_res, Acol, Bcol, ph_parent, l, small=False):
        with ExitStack() as ph:
            nsq = 2 if small else 4
            sq = [sb([128, S], BF16, "sq", ph) for _ in range(nsq)]
            sq_res = [Res("sq%d" % i) for i in range(nsq)]
            rstd = sb([128, S], F32, "rstd", ph)
            rstd_res = Res("rstd")
            if small:
                tmp = [sb([128, S], F32, "ntmp", ph)] * 2
                tmp_res = [Res("ntmp0")] * 2
            else:
                tmp = [sb([128, S], F32, "ntmp", ph) for _ in range(2)]
                tmp_res = [Res("ntmp0"), Res("ntmp1")]
            banks = [0, 1, 2, 3]
            for ci in range(8):
                si = ci % nsq
                if ci % 2 == 0:
                    c.op("act", lambda ci=ci, si=si: nc.scalar.activation(out=sq[si][:], in_=xT[:, ci, :], func=AF.Square),
                         reads=tuple(x_res[ci]), writes=(sq_res[si],))
                else:
                    c.op("dve", lambda ci=ci, si=si: nc.vector.tensor_tensor(out=sq[si][:], in0=xT[:, ci, :], in1=xT[:, ci, :], op=ALU.mult),
                         reads=tuple(x_res[ci]), writes=(sq_res[si],))
                for t in range(NT):
                    c.op("pe", lambda ci=ci, t=t, si=si: nc.tensor.matmul(
                        ps[:, banks[t], :], lhsT=ones_bf[:], rhs=sq[si][:, t * TT:(t + 1) * TT],
                        start=(ci == 0), stop=(ci == 7)),
                        reads=(sq_res[si], setup_res), writes=(bank_res[banks[t]],), inc=True)
            for t in range(NT):
                c.op("act", lambda t=t: nc.scalar.activation(out=rstd[:, t * TT:(t + 1) * TT], in_=ps[:, banks[t], :],
                                                           func=AF.Ln, bias=colv("eps"), scale=1.0 / D),
                     reads=(bank_res[banks[t]], cols_res), writes=(rstd_res,))
            c.op("act", lambda: nc.scalar.activation(out=rstd[:], in_=rstd[:], func=AF.Exp, scale=-0.5),
                 reads=(rstd_res,), writes=(rstd_res,))
            for ci in range(8):
                c.op("dve", lambda ci=ci: nc.vector.tensor_tensor(out=tmp[ci % 2][:], in0=xT[:, ci, :], in1=rstd[:],
                                                                  op=ALU.mult),
                     reads=tuple(x_res[ci]) + (rstd_res,), writes=(tmp_res[ci % 2],))
                c.op("act", lambda ci=ci: nc.scalar.activation(out=h[:, ci, :], in_=tmp[ci % 2][:], func=AF.Identity,
                                                             bias=Bcol(ci), scale=Acol(ci)),
                     reads=(tmp_res[ci % 2], mod_res_l[l], cols_res), writes=(h_res,))
            c.barrier()

    def proj_tile(wv, sres, k, col0, ncol, rhs_fn, rhs_res, nk, b, t0=None, n=TT, tok=None):
        for kk in range(nk):
            ws.check(sres, k)
            c.op("pe", lambda kk=kk: nc.tensor.matmul(
                ps[0:ncol, b, 0:n], lhsT=wv[:, kk, col0:col0 + ncol], rhs=rhs_fn(kk),
                start=(kk == 0), stop=(kk == nk - 1)),
                reads=(sres,) + tuple(rhs_res), writes=(bank_res[b],), inc=(kk == nk - 1))

    def emit_l0_mixer():
        l = 0
        with ExitStack() as ph:
            h = sb([128, 8, S], BF16, "h", ph)
            h_res = Res("h")
            y = sb([128, 8, S], BF16, "y", ph)
            y_res = Res("y")
            emit_norm(h, h_res, lambda ci: modA[:, l, 0, ci:ci + 1], lambda ci: modc[:, l, ci:ci + 1], ph, l)
            ring = Ring(list(range(8)))

            def wtile(i):
                return ws.get(lambda T, slot, i=i: [(
                    wview(slot, 8, 512),
                    T["ab_w_in"].rearrange("(k p) n -> p k n", p=128)[:, :, i * 512:(i + 1) * 512])])

            with ExitStack() as pa:
                cg = [sb([128, S], F32, "cg", pa) for _ in range(2)]
                cg_res = [Res("cg0"), Res("cg1")]
                u = [sb([128, S + 2], F32, "u", pa) for _ in range(2)]
                u_res = [Res("u0"), Res("u1")]
                cv = sb([128, S], F32, "cv", pa)
                cv_res = Res("cv")
                for i in range(2):
                    c.op("dve", lambda i=i: nc.vector.memset(u[i][:, 0:2], 0.0), writes=(u_res[i],))
                for i in range(4):
                    p2 = i % 2
                    s3, r3, k3 = ws.get(lambda T, slot, i=i: [
                        (wview(slot, 8, 384)[:, :, q * 128:(q + 1) * 128],
                         T["ab_w_in"].rearrange("(k p) n -> p k n", p=128)[:, :, sq_ * 512 + i * 128:sq_ * 512 + (i + 1) * 128])
                        for q, sq_ in enumerate((1, 2, 0))])
                    w3 = wview(s3, 8, 384)
                    for t in range(NT):
                        b = ring.next()
                        proj_tile(w3, r3, k3, 0, 128, lambda kk, t=t: h[:, kk, t * TT:(t + 1) * TT], (h_res,), 8, b)
                        c.op("act", lambda b=b, t=t, p2=p2: nc.scalar.activation(
                            out=cg[p2][:, t * TT:(t + 1) * TT], in_=ps[:, b, :], func=AF.Copy),
                            reads=(bank_res[b],), writes=(cg_res[p2],))
                    for t in range(NT):
                        b = ring.next()
                        proj_tile(w3, r3, k3, 128, 128, lambda kk, t=t: h[:, kk, t * TT:(t + 1) * TT], (h_res,), 8, b)
                        c.op("dve", lambda b=b, t=t, p2=p2: nc.vector.tensor_tensor(
                            out=u[p2][:, 2 + t * TT:2 + (t + 1) * TT], in0=ps[:, b, :], in1=cg[p2][:, t * TT:(t + 1) * TT],
                            op=ALU.mult),
                            reads=(bank_res[b], cg_res[p2]), writes=(u_res[p2],))
                    c.op("dve", lambda i=i, p2=p2: nc.vector.tensor_scalar(
                        out=cv[:], in0=u[p2][:, 2:S + 2], scalar1=colv("a_conv", i * 3 + 2), scalar2=None, op0=ALU.mult),
                        reads=(u_res[p2], cols_res), writes=(cv_res,))
                    for kq in (1, 0):
                        c.op("dve", lambda i=i, p2=p2, kq=kq: nc.vector.scalar_tensor_tensor(
                            out=cv[:], in0=u[p2][:, kq:S + kq], scalar=colv("a_conv", i * 3 + kq), in1=cv[:],
                            op0=ALU.mult, op1=ALU.add),
                            reads=(u_res[p2], cols_res, cv_res), writes=(cv_res,))
                    for t in range(NT):
                        b = ring.next()
                        proj_tile(w3, r3, k3, 256, 128, lambda kk, t=t: h[:, kk, t * TT:(t + 1) * TT], (h_res,), 8, b)
                        c.op("dve", lambda b=b, t=t, i=i: nc.vector.tensor_tensor(
                            out=y[:, i, t * TT:(t + 1) * TT], in0=ps[:, b, :], in1=cv[:, t * TT:(t + 1) * TT], op=ALU.mult),
                            reads=(bank_res[b], cv_res), writes=(y_res,))
                    bg_step(ring)
                    bg_step(ring)
                bg_flush(ring)
                c.barrier()
            with ExitStack() as pb:
                PADW = 16
                pbuf = [sb([128, S + PADW], F32, "pbuf", pb) for _ in range(3)]
                pb_res = [Res("pb0"), Res("pb1"), Res("pb2")]
                pl = [sb([128, S], BF16, "pl", pb) for _ in range(2)]
                pl_res = [Res("pl0"), Res("pl1")]
                t16 = sb([128, 16], F32, "t16", pb)
                t16_res = Res("t16")
                for i in range(3):
                    c.op("dve", lambda i=i: nc.vector.memset(pbuf[i][:, 0:PADW], 0.0), writes=(pb_res[i],))
                sp_, rp, kp = wtile(3)
                wp = wview(sp_, 8, 512)
                sm, rm, km = ws.get(lambda T, slot: [(wview(slot, 4, 128), T["b_mix_w"].rearrange("g c d -> c g d"))])
                wm = wview(sm, 4, 128)
                for g in range(4):
                    for t in range(NT):
                        b = ring.next()
                        proj_tile(wp, rp, kp, g * 128, 128, lambda kk, t=t: h[:, kk, t * TT:(t + 1) * TT], (h_res,), 8, b)
                        c.op("act", lambda b=b, t=t: nc.scalar.activation(
                            out=pbuf[0][:, PADW + t * TT:PADW + (t + 1) * TT], in_=ps[:, b, :], func=AF.Copy),
                            reads=(bank_res[b],), writes=(pb_res[0],))
                    src = 0
                    for lv in range(g + 1):
                        dst = 1 if src != 1 else 2
                        sh = 2 ** lv
                        c.op("dve", lambda src=src, dst=dst, sh=sh: nc.vector.tensor_tensor(
                            out=pbuf[dst][:, PADW:PADW + S], in0=pbuf[src][:, PADW:PADW + S],
                            in1=pbuf[src][:, PADW - sh:PADW + S - sh], op=ALU.add),
                            reads=(pb_res[src],), writes=(pb_res[dst],))
                        src = dst
                    w = float(2 ** (g + 1))
                    c.op("dve", lambda src=src, g=g, w=w: nc.vector.scalar_tensor_tensor(
                        out=pl[g % 2][:], in0=pbuf[src][:, PADW:PADW + S], scalar=1.0 / w, in1=pbuf[0][:, PADW:PADW + S],
                        op0=ALU.mult, op1=ALU.subtract),
                        reads=(pb_res[src], pb_res[0]), writes=(pl_res[g % 2],))
                    c.op("dve", lambda src=src, g=g: nc.vector.tensor_tensor(
                        out=t16[:], in0=pbuf[src][:, PADW:PADW + 16], in1=invc[:, g, :], op=ALU.mult),
                        reads=(pb_res[src], invc_res), writes=(t16_res,))
                    c.op("dve", lambda g=g: nc.vector.tensor_tensor(
                        out=pl[g % 2][:, 0:16], in0=t16[:], in1=pbuf[0][:, PADW:PADW + 16], op=ALU.subtract),
                        reads=(t16_res, pb_res[0], pl_res[g % 2]), writes=(pl_res[g % 2],))
                    for t in range(NT):
                        b = ring.next()
                        ws.check(rm, km)
                        c.op("pe", lambda b=b, t=t, g=g: nc.tensor.matmul(
                            ps[:, b, :], lhsT=wm[:, g, :], rhs=pl[g % 2][:, t * TT:(t + 1) * TT], start=True, stop=True),
                            reads=(rm, pl_res[g % 2]), writes=(bank_res[b],))
                        c.op("act", lambda b=b, t=t, g=g: nc.scalar.activation(
                            out=y[:, 4 + g, t * TT:(t + 1) * TT], in_=ps[:, b, :], func=AF.Copy, scale=colv("b_scale", g)),
                            reads=(bank_res[b], cols_res), writes=(y_res,))
                c.barrier()
            emit_out_proj("ab_w_out", lambda kk, t: y[:, kk, t * TT:(t + 1) * TT], (y_res,), l, ring)
            c.barrier()

    def emit_out_proj(wname, rhs_fn, rhs_res, l, ring):
        for half in range(2):
            so, ro, ko = ws.get(lambda T, slot, half=half: [(
                wview(slot, 8, 512),
                T[wname].rearrange("(k p) n -> p k n", p=128)[:, :, half * 512:(half + 1) * 512])])
            wo = wview(so, 8, 512)
            for dq in range(4):
                dc = half * 4 + dq
                for t in range(NT):
                    b = ring.next()
                    proj_tile(wo, ro, ko, dq * 128, 128, lambda kk, t=t: rhs_fn(kk, t), rhs_res, 8, b)
                    c.op("dve", lambda b=b, t=t, dc=dc: nc.vector.scalar_tensor_tensor(
                        out=xT[:, dc, t * TT:(t + 1) * TT], in0=ps[:, b, :], scalar=modc[:, l, 16 + dc:17 + dc],
                        in1=xT[:, dc, t * TT:(t + 1) * TT], op0=ALU.mult, op1=ALU.add),
                        reads=(bank_res[b], mod_res_l[l], x_res[dc][t]), writes=(x_res[dc][t],))

    def emit_l1_mixer():
        l = 1
        cdw = lambda T: T["cd_w_in"].rearrange("(k p) n -> p k n", p=128)
        with ExitStack() as ph0:
            yd = sb([128, 4, S], BF16, "yd", ph0)
            yd_res = Res("yd")
            mod_A(1, 0)
            with ExitStack() as ph:
                h = sb([128, 8, S], BF16, "h", ph)
                h_res = Res("h")
                emit_norm(h, h_res, lambda ci: modA[:, l, 0, ci:ci + 1], lambda ci: modc[:, l, ci:ci + 1], ph, l)
                u = sb([128, 4, S], BF16, "u", ph)
                u_res = Res("u")
                vn = sb([128, 16, 512], BF16, "vn", ph)
                vn_res = Res("vn")
                lnbc = sb([128, 2, 512], F32, "lnbc", ph)
                wsf = sb([128, 4, 128], F32, "wsf", ph)
                wsT = sb([128, 4, 128], BF16, "wsT", ph)
                bsr = sb([1, 4, 128], F32, "bsr", ph)
                ld_res = Res("l1ld")
                wsT_res = Res("wsT")
                vg = [sb([128, 512], F32, "vg", ph) for _ in range(3)]
                vg_res = [Res("vg0"), Res("vg1"), Res("vg2")]
                st6 = sb([128, 3, 6], F32, "st6", ph)
                mv = sb([128, 3, 2], F32, "mv", ph)
                st_res = [Res("st0"), Res("st1"), Res("st2")]
                c.sync_to("sp")
                c.dma("sp", lnbc[:], T["lnbc"], writes=(ld_res,))
                c.dma("sp", wsf[:], T["d_w_sT"].rearrange("g s t -> s g t"), writes=(ld_res,))
                c.dma("sp", bsr[:], T["bs"], writes=(ld_res,))
                for g in range(4):
                    c.op("dve", lambda g=g: nc.vector.tensor_tensor(out=wsT[:, g, :], in0=wsf[:, g, :], in1=tril[:], op=ALU.mult),
                         reads=(ld_res, setup_res), writes=(wsT_res,))
                ring = Ring([0, 1, 2, 3, 4, 5])
                su, ru, ku = ws.get(lambda T, slot: [(wview(slot, 8, 512), cdw(T)[:, :, 416:928])])
                wu = wview(su, 8, 512)
                for i in range(4):
                    for t in range(NT):
                        b = ring.next()
                        proj_tile(wu, ru, ku, i * 128, 128, lambda kk, t=t: h[:, kk, t * TT:(t + 1) * TT], (h_res,), 8, b)
                        c.op("act", lambda b=b, i=i, t=t: nc.scalar.activation(
                            out=u[:, i, t * TT:(t + 1) * TT], in_=ps[:, b, :], func=AF.Gelu_apprx_tanh),
                            reads=(bank_res[b],), writes=(u_res,))
                sv, rv, kv_ = ws.get(lambda T, slot: [(wview(slot, 8, 512), cdw(T)[:, :, 928:1440])])
                wv_ = wview(sv, 8, 512)
                mhalf = sb([128, 1], F32, "mhalf", ph)
                mh_res = Res("mhalf")
                c.op("dve", lambda: nc.vector.memset(mhalf[:], -0.5), writes=(mh_res,))
                ve = sb([128, 3, 1], F32, "ve", ph)
                rsd = sb([128, 3, 1], F32, "rsd", ph)
                ve_res = [Res("ve%d" % i) for i in range(3)]
                rsd_res = [Res("rsd%d" % i) for i in range(3)]

                def v_stats(tb):
                    b = ring.next()
                    p3 = tb % 3
                    for kk in range(8):
                        ws.check(rv, kv_)
                        c.op("pe", lambda kk=kk: nc.tensor.matmul(
                            ps[:, b, :], lhsT=h[:, kk, tb * 128:(tb + 1) * 128], rhs=wv_[:, kk, :],
                            start=(kk == 0), stop=(kk == 7)),
                            reads=(rv, h_res), writes=(bank_res[b],), inc=(kk == 7))
                    c.op("act", lambda: nc.scalar.activation(out=vg[p3][:], in_=ps[:, b, :], func=AF.Gelu_apprx_tanh),
                         reads=(bank_res[b],), writes=(vg_res[p3],))
                    c.op("dve", lambda: nc.vector.bn_stats(out=st6[:, p3, :], in_=vg[p3][:]),
                         reads=(vg_res[p3],), writes=(st_res[p3],))
                    c.op("dve", lambda: nc.vector.bn_aggr(out=mv[:, p3, :], in_=st6[:, p3, :]),
                         reads=(st_res[p3],), writes=(st_res[p3],))
                    c.op("dve", lambda: nc.vector.tensor_scalar(out=ve[:, p3, :], in0=mv[:, p3, 1:2], scalar1=EPS, scalar2=None,
                                                                op0=ALU.add),
                         reads=(st_res[p3],), writes=(ve_res[p3],))
                    c.op("pool", lambda: nc.gpsimd.tensor_tensor(out=rsd[:, p3, :], in0=ve[:, p3, :], in1=mhalf[:], op=ALU.pow),
                         reads=(ve_res[p3], mh_res), writes=(rsd_res[p3],))

                def v_norm(tb):
                    p3 = tb % 3
                    c.op("dve", lambda: nc.vector.tensor_scalar(
                        out=vg[p3][:], in0=vg[p3][:], scalar1=mv[:, p3, 0:1], scalar2=rsd[:, p3, :],
                        op0=ALU.subtract, op1=ALU.mult),
                        reads=(vg_res[p3], st_res[p3], rsd_res[p3]), writes=(vg_res[p3],))
                    c.op("dve", lambda: nc.vector.tensor_tensor(out=vg[p3][:], in0=vg[p3][:], in1=lnbc[:, 0, :], op=ALU.mult),
                         reads=(vg_res[p3], ld_res), writes=(vg_res[p3],))
                    c.op("dve", lambda: nc.vector.tensor_tensor(out=vn[:, tb, :], in0=vg[p3][:], in1=lnbc[:, 1, :], op=ALU.add),
                         reads=(vg_res[p3], ld_res), writes=(vn_res,))

                v_stats(0)
                for tb in range(1, 16):
                    v_stats(tb)
                    v_norm(tb - 1)
                v_norm(15)
                for g in range(4):
                    for t in range(NT):
                        b = ring.next()
                        for q in range(4):
                            n = t * 4 + q
                            c.op("pe", lambda g=g, n=n, q=q, b=b: nc.tensor.matmul(
                                ps[:, b, q * 128:(q + 1) * 128], lhsT=vn[:, n, g * 128:(g + 1) * 128], rhs=wsT[:, g, :],
                                start=True, stop=False),
                                reads=(vn_res, wsT_res), writes=(bank_res[b],), inc=False)
                            c.op("pe", lambda g=g, q=q, b=b: nc.tensor.matmul(
                                ps[:, b, q * 128:(q + 1) * 128], lhsT=ones_f[0:1, :], rhs=bsr[0:1, g, :],
                                start=False, stop=True),
                                reads=(ld_res, setup_res), writes=(bank_res[b],), inc=(q == 3))
                        c.op("dve", lambda g=g, t=t, b=b: nc.vector.tensor_tensor(
                            out=yd[:, g, t * TT:(t + 1) * TT], in0=ps[:, b, :], in1=u[:, g, t * TT:(t + 1) * TT], op=ALU.mult),
                            reads=(bank_res[b], u_res), writes=(yd_res,))
                c.barrier()
            Ct = sb([128, S], F32, "Ct", ph0)
            Sg = sb([128, S], F32, "Sg", ph0)
            tab_res = Res("tab")
            qn = sb([128, 2, S], BF16, "qn", ph0)
            qn_res = Res("qn")
            kvn = sb([128, S], BF16, "kvn", ph0)
            kvn_res = Res("kvn")
            kpe = sb([128, S], BF16, "kpe", ph0)
            kpe_res = Res("kpe")
            c.sync_to("sp")
            nc.sync.wait_ge(tabw_res.dsem, tabw_res.dcnt)
            c.dma("sp", Ct[:], tab_d[0], writes=(tab_res,))
            c.dma("sp", Sg[:], tab_d[1], writes=(tab_res,))
            with ExitStack() as ph:
                h = sb([128, 8, S], BF16, "h", ph)
                h_res = Res("h")
                emit_norm(h, h_res, lambda ci: modA[:, l, 0, ci:ci + 1], lambda ci: modc[:, l, ci:ci + 1], ph, l, small=True)
                sqt = [sb([128, TT], BF16, "sqt", ph) for _ in range(2)]
                sqt_res = [Res("sqt0"), Res("sqt1")]
                rs = [sb([128, TT], F32, "rs", ph) for _ in range(2)]
                rs_res = [Res("rs0"), Res("rs1")]
                t12 = [sb([128, TT], F32, "t12", ph) for _ in range(2)]
                t12_res = [Res("t1"), Res("t2")]
                ring = Ring([0, 1, 2, 3, 4, 5, 6, 7])
                s1, r1, k1 = ws.get(lambda T, slot: [(wview(slot, 8, 416), cdw(T)[:, :, 0:416])])
                w1 = wview(s1, 8, 416)
                s2, r2, k2 = ws.get(lambda T, slot: [
                    (wview(slot, 8, 192)[:, :, 64:96], cdw(T)[:, :, 384:416]),
                    (wview(slot, 8, 192)[:, :, 96 + 64:96 + 80], cdw(T)[:, :, 400:416]),
                    (wview(slot, 8, 192)[:, :, 96 + 80:96 + 96], cdw(T)[:, :, 384:400])])
                w2 = wview(s2, 8, 192)
                hr = lambda t: (lambda kk: h[:, kk, t * TT:(t + 1) * TT])

                def rms_from_banks(banks, nfeat, gname, dsts, t, par):
                    bsum = ring.next()
                    for i, b in enumerate(banks):
                        c.op("act", lambda b=b, i=i: nc.scalar.activation(out=sqt[i % 2][:], in_=ps[:, b, :], func=AF.Square),
                             reads=(bank_res[b],), writes=(sqt_res[i % 2],))
                        c.op("pe", lambda i=i, bsum=bsum: nc.tensor.matmul(
                            ps[:, bsum, :], lhsT=ones_bf[:], rhs=sqt[i % 2][:], start=(i == 0), stop=(i == len(banks) - 1)),
                            reads=(sqt_res[i % 2], setup_res), writes=(bank_res[bsum],), inc=True)
                    c.op("act", lambda: nc.scalar.activation(out=rs[par][:], in_=ps[:, bsum, :], func=AF.Ln,
                                                           bias=colv("eps"), scale=1.0 / nfeat),
                         reads=(bank_res[bsum], cols_res), writes=(rs_res[par],))
                    c.op("act", lambda: nc.scalar.activation(out=rs[par][:], in_=rs[par][:], func=AF.Exp, scale=-0.5),
                         reads=(rs_res[par],), writes=(rs_res[par],))
                    for i, b in enumerate(banks):
                        dst, dres = dsts[i]
                        c.op("dve", lambda b=b, i=i, dst=dst: nc.vector.scalar_tensor_tensor(
                            out=dst, in0=ps[:, b, :], scalar=colv(gname, i), in1=rs[par][:], op0=ALU.mult, op1=ALU.mult),
                            reads=(bank_res[b], cols_res, rs_res[par]), writes=(dres,))

                for t in range(NT):
                    sl = slice(t * TT, (t + 1) * TT)
                    bq = [ring.next(), ring.next()]
                    for cc in range(2):
                        proj_tile(w1, r1, k1, cc * 128, 128, hr(t), (h_res,), 8, bq[cc])
                    bk = ring.next()
                    proj_tile(w1, r1, k1, 256, 128, hr(t), (h_res,), 8, bk)
                    b1, b2 = ring.next(), ring.next()
                    proj_tile(w2, r2, k2, 0, 96, hr(t), (h_res,), 8, b1)
                    proj_tile(w2, r2, k2, 96, 96, hr(t), (h_res,), 8, b2)
                    rms_from_banks(bq, 256.0, "q_norm_g", [(qn[:, 0, sl], qn_res), (qn[:, 1, sl], qn_res)], t, 0)
                    rms_from_banks([bk], 128.0, "kv_norm_g", [(kvn[:, sl], kvn_res)], t, 1)
                    c.op("dve", lambda: nc.vector.tensor_tensor(out=t12[0][64:96, :], in0=ps[64:96, b1, :], in1=Ct[64:96, sl], op=ALU.mult),
                         reads=(bank_res[b1], tab_res), writes=(t12_res[0],))
                    c.op("dve", lambda: nc.vector.tensor_tensor(out=t12[1][64:96, :], in0=ps[64:96, b2, :], in1=Sg[64:96, sl], op=ALU.mult),
                         reads=(bank_res[b2], tab_res), writes=(t12_res[1],))
                    c.op("dve", lambda: nc.vector.tensor_tensor(out=kpe[64:96, sl], in0=t12[0][64:96, :], in1=t12[1][64:96, :], op=ALU.add),
                         reads=(t12_res[0], t12_res[1]), writes=(kpe_res,))
                c.barrier()
            with ExitStack() as ph:
                yc = sb([128, 4, S], BF16, "yc", ph)
                yc_res = Res("yc")
                QT = [sb([128, S], BF16, "QT", ph) for _ in range(2)]
                KT = [sb([128, S], BF16, "KT", ph) for _ in range(2)]
                QT_res = [Res("QT0"), Res("QT1")]
                KT_res = [Res("KT0"), Res("KT1")]
                QTp_res = [Res("QTp0"), Res("QTp1")]
                KTp_res = [Res("KTp0"), Res("KTp1")]
                Va = [sb([128, 16, 128], BF16, "Va", ph) for _ in range(2)]
                Va_res = [Res("Va0"), Res("Va1")]
                pt = [sb([128, TT], BF16, "pt", ph) for _ in range(5)]
                pt_res = [Res("pt%d" % i) for i in range(5)]
                t12 = [sb([128, TT], F32, "t12", ph) for _ in range(2)]
                t12_res = [Res("t1"), Res("t2")]
                rt = sb([128, TT], F32, "rt", ph)
                rt_res = Res("rt")
                c.op("dve", lambda: nc.vector.memset(Va[0][:, :, 64:128], 1.0), writes=(Va_res[0],))
                c.op("dve", lambda: nc.vector.memset(Va[1][:, :, 0:64], 1.0), writes=(Va_res[1],))
                sq_, rq, kq_ = ws.get(lambda T, slot: [
                    (wview(slot, 2, 768), T["c_w_uq"].rearrange("(k p) n -> p k n", p=128))
                ] + [
                    (slot[:, 1536 + kk * 768:1536 + (kk + 1) * 768].rearrange("p (h e) -> p h e", h=8)[:, :, d0:d0 + 16],
                     T["c_w_uq"][kk * 128:(kk + 1) * 128, :].rearrange("p (h e) -> p h e", e=96)[:, :, s0:s0 + 16])
                    for kk in range(2) for (d0, s0) in ((64, 80), (80, 64))])
                wqa = wview(sq_, 2, 768)
                wqb = sq_[:, 1536:3072].rearrange("p (k n) -> p k n", k=2)
                skv, rkv, kkv = ws.get(lambda T, slot: [(slot[:, 0:1024], T["c_w_ukv"])])
                wkv = skv[:, 0:1024]
                sring = Ring([0, 1, 2, 3])
                pring = Ring([4, 5])
                aring = Ring([6, 7])
                ptring = Ring([0, 1, 2, 3, 4])
                scale = float(96.0 ** -0.5)

                def prep_pieces(hd):
                    p2 = hd % 2
                    pieces = []

                    def piece_t(t):
                        sl = slice(t * TT, (t + 1) * TT)
                        ba, bb = pring.next(), pring.next()
                        for (wq_, bnk) in ((wqa, ba), (wqb, bb)):
                            for kk in range(2):
                                ws.check(rq, kq_)
                                c.op("pe", lambda kk=kk, wq_=wq_, bnk=bnk: nc.tensor.matmul(
                                    ps[0:96, bnk, :], lhsT=wq_[:, kk, hd * 96:(hd + 1) * 96], rhs=qn[:, kk, sl],
                                    start=(kk == 0), stop=(kk == 1)),
                                    reads=(rq, qn_res), writes=(bank_res[bnk],), inc=(kk == 1))
                        c.op("dve", lambda: nc.vector.tensor_copy(out=QT[p2][0:64, sl], in_=ps[0:64, ba, :]),
                             reads=(bank_res[ba],), writes=(QT_res[p2],))
                        c.op("dve", lambda: nc.vector.tensor_tensor(out=t12[0][64:96, :], in0=ps[64:96, ba, :], in1=Ct[64:96, sl], op=ALU.mult),
                             reads=(bank_res[ba], tab_res), writes=(t12_res[0],))
                        c.op("dve", lambda: nc.vector.tensor_tensor(out=t12[1][64:96, :], in0=ps[64:96, bb, :], in1=Sg[64:96, sl], op=ALU.mult),
                             reads=(bank_res[bb], tab_res), writes=(t12_res[1],))
                        c.op("dve", lambda: nc.vector.tensor_tensor(out=QT[p2][64:96, sl], in0=t12[0][64:96, :], in1=t12[1][64:96, :], op=ALU.add),
                             reads=(t12_res[0], t12_res[1]), writes=(QTp_res[p2],))
                        bk = pring.next()
                        ws.check(rkv, kkv)
                        c.op("pe", lambda: nc.tensor.matmul(ps[0:64, bk, :], lhsT=wkv[:, hd * 128:hd * 128 + 64], rhs=kvn[:, sl],
                                                            start=True, stop=True),
                             reads=(rkv, kvn_res), writes=(bank_res[bk],))
                        c.op("dve", lambda: nc.vector.tensor_copy(out=KT[p2][0:64, sl], in_=ps[0:64, bk, :]),
                             reads=(bank_res[bk],), writes=(KT_res[p2],))

                    def piece_kpe():
                        c.op("dve", lambda: nc.vector.tensor_copy(out=KT[p2][64:96, :], in_=kpe[64:96, :]),
                             reads=(kpe_res,), writes=(KTp_res[p2],))

                    voff = 0 if p2 == 0 else 64

                    def piece_v(half):
                        bv = pring.next()
                        for q in range(8):
                            kb = half * 8 + q
                            ws.check(rkv, kkv)
                            c.op("pe", lambda kb=kb, q=q: nc.tensor.matmul(
                                ps[:, bv, q * 64:(q + 1) * 64], lhsT=kvn[:, kb * 128:(kb + 1) * 128],
                                rhs=wkv[:, hd * 128 + 64:hd * 128 + 128], start=True, stop=True),
                                reads=(rkv, kvn_res), writes=(bank_res[bv],), inc=(q == 7))
                        c.op("dve", lambda: nc.vector.tensor_copy(
                            out=Va[p2][:, half * 8:half * 8 + 8, voff:voff + 64],
                            in_=ps[:, bv, :].rearrange("p (q e) -> p q e", q=8)),
                            reads=(bank_res[bv],), writes=(Va_res[p2],))

                    for t in range(NT):
                        pieces.append(lambda t=t: piece_t(t))
                    pieces.append(piece_kpe)
                    pieces.append(lambda: piece_v(0))
                    pieces.append(lambda: piece_v(1))
                    return pieces

                LOOK = 3

                def attn(hd, pieces):
                    p2 = hd % 2
                    steps = []
                    for qt in range(NT):
                        nkb = 4 * qt + 4
                        for kb in range(nkb):
                            if kb < 4 * qt:
                                q0, n = qt * TT, TT
                            else:
                                q0 = kb * 128
                                n = (qt + 1) * TT - q0
                            steps.append((qt, kb, nkb, q0, n))
                    state = {}
                    pending = []

                    def score(i):
                        qt, kb, nkb, q0, n = steps[i]
                        sb_ = sring.next()
                        c.op("pe", lambda: nc.tensor.matmul(
                            ps[:, sb_, 0:n], lhsT=KT[p2][0:96, kb * 128:(kb + 1) * 128], rhs=QT[p2][0:96, q0:q0 + n],
                            start=True, stop=True),
                            reads=(KT_res[p2], QT_res[p2], KTp_res[p2], QTp_res[p2]), writes=(bank_res[sb_],))
                        pi_ = ptring.next()
                        c.op("act", lambda: nc.scalar.activation(out=pt[pi_][:, 0:n], in_=ps[:, sb_, 0:n], func=AF.Exp, scale=scale),
                             reads=(bank_res[sb_],), writes=(pt_res[pi_],))
                        if kb >= 4 * qt:
                            c.op("dve", lambda: nc.vector.tensor_tensor(out=pt[pi_][:, 0:128], in0=pt[pi_][:, 0:128], in1=tril[:], op=ALU.mult),
                                 reads=(pt_res[pi_], setup_res), writes=(pt_res[pi_],))
                        state[i] = pi_

                    def pv(i):
                        qt, kb, nkb, q0, n = steps[i]
                        if kb == 0:
                            state["ab"] = aring.next()
                        ab = state["ab"]
                        pi_ = state.pop(i)
                        o0 = q0 - qt * TT
                        c.op("pe", lambda: nc.tensor.matmul(
                            ps[:, ab, o0:o0 + n], lhsT=Va[p2][:, kb, :], rhs=pt[pi_][:, 0:n],
                            start=(kb == 0), stop=(kb == nkb - 1)),
                            reads=(Va_res[p2], pt_res[pi_]), writes=(bank_res[ab],))
                        if kb == nkb - 1:
                            pending.append((i + 4, lambda: normalize(qt, ab)))

                    def normalize(qt, ab):
                        if True:
                            sl = slice(qt * TT, (qt + 1) * TT)
                            if p2 == 0:
                                c.op("act", lambda: nc.scalar.activation(out=rt[0:64, :], in_=ps[64:128, ab, :], func=AF.Ln),
                                     reads=(bank_res[ab],), writes=(rt_res,))
                                c.op("act", lambda: nc.scalar.activation(out=rt[0:64, :], in_=rt[0:64, :], func=AF.Exp, scale=-1.0),
                                     reads=(rt_res,), writes=(rt_res,))
                                c.op("dve", lambda: nc.vector.tensor_tensor(out=yc[0:64, hd // 2, sl], in0=ps[0:64, ab, :], in1=rt[0:64, :], op=ALU.mult),
                                     reads=(bank_res[ab], rt_res), writes=(yc_res,))
                            else:
                                c.op("act", lambda: nc.scalar.activation(out=rt[64:128, :], in_=ps[0:64, ab, :], func=AF.Ln),
                                     reads=(bank_res[ab],), writes=(rt_res,))
                                c.op("act", lambda: nc.scalar.activation(out=rt[64:128, :], in_=rt[64:128, :], func=AF.Exp, scale=-1.0),
                                     reads=(rt_res,), writes=(rt_res,))
                                c.op("dve", lambda: nc.vector.tensor_tensor(out=yc[64:128, hd // 2, sl], in0=ps[64:128, ab, :], in1=rt[64:128, :], op=ALU.mult),
                                     reads=(bank_res[ab], rt_res), writes=(yc_res,))

                    ns = len(steps)
                    every = max(1, ns // (len(pieces) + 1)) if pieces else ns + 1
                    for i in range(min(LOOK, ns)):
                        score(i)
                    for i in range(ns):
                        if i + LOOK < ns:
                            score(i + LOOK)
                        pv(i)
                        while pending and pending[0][0] <= i:
                            pending.pop(0)[1]()
                        if pieces and (i + 1) % every == 0:
                            pieces.pop(0)()
                    while pending:
                        pending.pop(0)[1]()
                    while pieces:
                        pieces.pop(0)()

                for pc in prep_pieces(0):
                    pc()
                for hd in range(8):
                    attn(hd, prep_pieces(hd + 1) if hd + 1 < 8 else [])
                c.barrier()
                ring = Ring(list(range(8)))
                emit_out_proj("cd_w_out", lambda kk, t: (yc[:, kk, t * TT:(t + 1) * TT] if kk < 4 else yd[:, kk - 4, t * TT:(t + 1) * TT]),
                              (yc_res, yd_res), l, ring)
                c.barrier()

    def emit_ffn(l):
        with ExitStack() as ph:
            h = sb([128, 8, S], BF16, "h", ph)
            h_res = Res("h")
            mod_A(l, 1)
            if l == 0 and upto >= 3:
                for i in range(12):
                    bg.append(lambda b, i=i: mod_tile(1, i, b))
            emit_norm(h, h_res, lambda ci: modA[:, l, 1, ci:ci + 1], lambda ci: modc[:, l, 24 + ci:25 + ci], ph, l)
            acc = [[sb([128, S], F32, "acc", ph) for _ in range(2)] for _ in range(2)]
            acc_res = [[[Res("acc%d%d_%d" % (p, q, ti)) for ti in range(len(FFN_TILES))] for q in range(2)] for p in range(2)]
            act = [sb([128, JG, S], BF16, "act", ph) for _ in range(2)]
            act_res = [Res("act0"), Res("act1")]
            zring = Ring([0, 1, 2, 3, 4])
            dring = Ring([5, 6, 7])
            groups = [list(range(g0, min(g0 + JG, NPAIR))) for g0 in range(0, NPAIR, JG)]

            def emit_down(gi):
                grp = groups[gi]
                ng = len(grp)
                j0 = grp[0]
                sd, rd, kd = ws.get(lambda T, slot, j0=j0, ng=ng: [(
                    wview(slot, ng, 1024),
                    T["ffn_w_down"][l][j0 * 128:(j0 + ng) * 128, :].rearrange("(j p) d -> p j d", p=128))])
                wd = wview(sd, ng, 1024)
                a = act[gi % 2]
                ar = act_res[gi % 2]
                for dc in range(8):
                    for t in range(NT):
                        b = dring.next()
                        for jj in range(ng):
                            ws.check(rd, kd)
                            c.op("pe", lambda jj=jj, dc=dc, t=t, b=b: nc.tensor.matmul(
                                ps[:, b, :], lhsT=wd[:, jj, dc * 128:(dc + 1) * 128], rhs=a[:, jj, t * TT:(t + 1) * TT],
                                start=(jj == 0), stop=(jj == ng - 1)),
                                reads=(rd, ar), writes=(bank_res[b],), inc=(jj == ng - 1))
                        c.op("dve", lambda b=b, t=t, dc=dc: nc.vector.scalar_tensor_tensor(
                            out=xT[:, dc, t * TT:(t + 1) * TT], in0=ps[:, b, :], scalar=modc[:, l, 40 + dc:41 + dc],
                            in1=xT[:, dc, t * TT:(t + 1) * TT], op0=ALU.mult, op1=ALU.add),
                            reads=(bank_res[b], mod_res_l[l], x_res[dc][t]), writes=(x_res[dc][t],))

            for gi, grp in enumerate(groups):
                for jj, j in enumerate(grp):
                    par = j % 2
                    su, ru, ku = ws.get(lambda T, slot, j=j: [
                        (wview(slot, 8, 256)[:, :, 0:128],
                         T["ffn_w_up"][l].rearrange("(k p) n -> p k n", p=128)[:, :, j * 128:(j + 1) * 128]),
                        (wview(slot, 8, 256)[:, :, 128:256],
                         T["ffn_w_up"][l].rearrange("(k p) n -> p k n", p=128)[:, :, DFF + j * 128:DFF + (j + 1) * 128])])
                    wu = wview(su, 8, 256)
                    for br in range(2):
                        A = acc[par][br]
                        cw = COLOFF["ffn_conv"] + ((l * 2 + br) * NPAIR + j) * 3
                        for ti, (t0, n, o0, m, off) in enumerate(FFN_TILES):
                            Ar = acc_res[par][br][ti]
                            b = zring.next()
                            proj_tile(wu, ru, ku, br * 128, 128, lambda kk, t0=t0, n=n: h[:, kk, t0:t0 + n], (h_res,), 8, b, n=n)
                            c.op("act", lambda b=b, o0=o0, m=m, off=off, A=A, cw=cw: nc.scalar.activation(
                                out=A[:, o0:o0 + m], in_=ps[:, b, off:off + m], func=AF.Copy, scale=cols[:, cw + 2:cw + 3]),
                                reads=(bank_res[b], cols_res), writes=(Ar,))
                            for sh in (1, 2):
                                if off == 0:
                                    oo, mm_, so = o0 + sh, m - sh, 0
                                else:
                                    oo, mm_, so = o0, m, off - sh
                                c.op("dve", lambda b=b, oo=oo, mm_=mm_, so=so, A=A, cw=cw, sh=sh: nc.vector.scalar_tensor_tensor(
                                    out=A[:, oo:oo + mm_], in0=ps[:, b, so:so + mm_], scalar=cols[:, cw + 2 - sh:cw + 3 - sh],
                                    in1=A[:, oo:oo + mm_], op0=ALU.mult, op1=ALU.add),
                                    reads=(bank_res[b], cols_res, Ar), writes=(Ar,))
                    c.op("act", lambda par=par: nc.scalar.activation(out=acc[par][0][:], in_=acc[par][0][:], func=AF.Silu),
                         reads=tuple(acc_res[par][0]), writes=tuple(acc_res[par][0]))
                    c.op("dve", lambda par=par, gi=gi, jj=jj: nc.vector.tensor_tensor(
                        out=act[gi % 2][:, jj, :], in0=acc[par][0][:], in1=acc[par][1][:], op=ALU.mult),
                        reads=tuple(acc_res[par][0]) + tuple(acc_res[par][1]), writes=(act_res[gi % 2],))
                    if jj == 0 and gi > 0:
                        emit_down(gi - 1)
                    bg_step(dring)
            emit_down(len(groups) - 1)
            bg_flush(dring)
            c.barrier()

    def emit_output(final_norm):
        with ExitStack() as ph:
            ostg = [sb([128, D], F32, "ostg", ph) for _ in range(2)]
            ostg_res = [Res("ostg0"), Res("ostg1")]
            out_res = Res("out")
            if final_norm:
                gbc = sb([128, D], F32, "gbc", ph)
                gbc_res = Res("gbc")
                c.sync_to("sp")
                c.dma("sp", gbc[:], T["fgbc"], writes=(gbc_res,))
                junk = [sb([128, 512], BF16, "junk", ph) for _ in range(2)]
                junk_res = [Res("junk0"), Res("junk1")]
                ss = sb([128, 2, 2], F32, "ss", ph)
                rs1 = sb([128, 2, 1], F32, "rs1", ph)
                ss_res = [Res("ss0"), Res("ss1")]
                mhalf = sb([128, 1], F32, "mhalf", ph)
                mh_res = Res("mhalf")
                c.op("dve", lambda: nc.vector.memset(mhalf[:], -0.5), writes=(mh_res,))
            ring = Ring([0, 1, 2, 3, 4, 5, 6, 7])
            for tb in range(16):
                tt = tb // 4
                si = tb % 2
                bs_ = []
                for half in range(2):
                    b = ring.next()
                    bs_.append(b)
                    for q in range(4):
                        ci = half * 4 + q
                        c.op("pe", lambda ci=ci, b=b, q=q, tb=tb: nc.tensor.transpose(
                            ps[:, b, q * 128:(q + 1) * 128], xT[:, ci, tb * 128:(tb + 1) * 128], ident[:]),
                            reads=(x_res[ci][tt], setup_res), writes=(bank_res[b],), inc=(q == 3))
                if final_norm:
                    for half in range(2):
                        b = bs_[half]
                        c.op("act", lambda half=half, b=b: nc.scalar.activation(
                            out=junk[half][:], in_=ps[:, b, :], func=AF.Square, accum_out=ss[:, si, half:half + 1]),
                            reads=(bank_res[b],), writes=(junk_res[half], ss_res[si]))
                    c.op("dve", lambda: nc.vector.tensor_tensor(out=rs1[:, si, :], in0=ss[:, si, 0:1], in1=ss[:, si, 1:2], op=ALU.add),
                         reads=(ss_res[si],), writes=(ss_res[si],))
                    c.op("dve", lambda: nc.vector.tensor_scalar(out=rs1[:, si, :], in0=rs1[:, si, :], scalar1=1.0 / D, scalar2=EPS,
                                                                op0=ALU.mult, op1=ALU.add),
                         reads=(ss_res[si],), writes=(ss_res[si],))
                    c.op("pool", lambda: nc.gpsimd.tensor_tensor(out=rs1[:, si, :], in0=rs1[:, si, :], in1=mhalf[:], op=ALU.pow),
                         reads=(ss_res[si], mh_res), writes=(ss_res[si],))
                    for half in range(2):
                        b = bs_[half]
                        c.op("dve", lambda half=half, b=b: nc.vector.scalar_tensor_tensor(
                            out=ostg[si][:, half * 512:(half + 1) * 512], in0=ps[:, b, :], scalar=rs1[:, si, :],
                            in1=gbc[:, half * 512:(half + 1) * 512], op0=ALU.mult, op1=ALU.mult),
                            reads=(bank_res[b], ss_res[si], gbc_res), writes=(ostg_res[si],))
                else:
                    for half in range(2):
                        b = bs_[half]
                        dst = ostg[si][:, half * 512:(half + 1) * 512]
                        if half == 0:
                            c.op("act", lambda dst=dst, b=b: nc.scalar.activation(out=dst, in_=ps[:, b, :], func=AF.Copy),
                                 reads=(bank_res[b],), writes=(ostg_res[si],))
                        else:
                            c.op("dve", lambda dst=dst, b=b: nc.vector.tensor_copy(out=dst, in_=ps[:, b, :]),
                                 reads=(bank_res[b],), writes=(ostg_res[si],))
                c.dma("sp", out_d[tb * 128:(tb + 1) * 128, :], ostg[si][:], reads=(ostg_res[si],), writes=(), done=out_res)
            nc.sync.wait_ge(out_res.dsem, out_res.dcnt)

    if upto >= 1:
        for i in range(4):
            mod_tile(0, i, i)
        mod_A(0, 0)
        for i in range(4, 12):
            bg.append(lambda b, i=i: mod_tile(0, i, b))
        emit_l0_mixer()
    if upto >= 2:
        emit_ffn(0)
    if upto >= 3:
        emit_l1_mixer()
    if upto >= 4:
        emit_ffn(1)
    emit_output(final_norm=(stop is None))
    es.close()
    return nc, ws.rec


COLSPEC = [("c", 8), ("ada_b", 96), ("norm1_g", 16), ("norm2_g", 16), ("final_g", 8), ("a_conv", 12),
           ("b_scale", 4), ("q_norm_g", 2), ("kv_norm_g", 1), ("ffn_conv", 2 * 2 * NPAIR * 3),
           ("inv_freq", 1), ("rope_sign", 1), ("eps", 1), ("pi", 1)]
COLOFF = {}
_o = 0
for _n, _w in COLSPEC:
    COLOFF[_n] = _o
    _o += _w
NCOLS = _o


def colify(v):
    v = np.asarray(v, dtype=np.float32)
    return np.ascontiguousarray(v.reshape(-1, 128).T)


def make_cols(b, inp):
    cols = np.zeros((128, NCOLS), np.float32)

    def put(name, arr):
        o = COLOFF[name]
        cols[:, o:o + arr.shape[1]] = arr

    put("c", colify(inp["c"][b]))
    put("ada_b", np.concatenate([colify(inp["ada_b"][l]) for l in range(2)], axis=1))
    put("norm1_g", np.concatenate([colify(inp["norm1_g"][l]) for l in range(2)], axis=1))
    put("norm2_g", np.concatenate([colify(inp["norm2_g"][l]) for l in range(2)], axis=1))
    put("final_g", colify(inp["final_norm_g"]))
    ac = np.asarray(inp["a_conv_w"][0], np.float32)
    put("a_conv", np.ascontiguousarray(ac.reshape(3, 4, 128).transpose(2, 1, 0).reshape(128, 12)))
    put("b_scale", colify(inp["b_scale"][0]))
    put("q_norm_g", colify(inp["c_q_norm_g"][0]))
    put("kv_norm_g", colify(inp["c_kv_norm_g"][0]))
    fc = np.asarray(inp["ffn_conv_w"], np.float32)
    fc = fc.reshape(2, 3, 2, NPAIR, 128).transpose(4, 0, 2, 3, 1).reshape(128, -1)
    put("ffn_conv", np.ascontiguousarray(fc))
    half = 16
    inv_freq = (np.float32(10000.0) ** (-(np.arange(half, dtype=np.float32) / np.float32(half)))).astype(np.float32)
    p = np.arange(128)
    put("inv_freq", inv_freq[p % 16][:, None])
    put("rope_sign", np.where((p % 32) < 16, -1.0, 1.0).astype(np.float32)[:, None])
    put("eps", np.full((128, 1), EPS, np.float32))
    put("pi", np.full((128, 1), np.pi, np.float32))
    return cols


_CACHE = {}


def get_program(stop=None):
    if stop not in _CACHE:
        _, rec = build(plan=None, stop=stop)
        nc, _ = build(plan=rec, stop=stop)
        _CACHE[stop] = nc
    return _CACHE[stop]


def make_in_maps(inp, cores):
    f = lambda a: np.ascontiguousarray(np.asarray(a, dtype=np.float32))
    shared = {
        "ada_w": f(inp["ada_w"]),
        "ab_w_in": f(inp["ab_w_in"][0]),
        "b_mix_w": f(inp["b_mix_w"][0]),
        "ab_w_out": f(inp["ab_w_out"][0]),
        "cd_w_in": f(inp["cd_w_in"][0]),
        "c_w_uq": f(inp["c_w_uq"][0]),
        "c_w_ukv": f(inp["c_w_ukv"][0]),
        "d_w_sT": f(np.transpose(np.asarray(inp["d_w_s"][0]), (0, 2, 1))),
        "cd_w_out": f(inp["cd_w_out"][0]),
        "ffn_w_up": f(inp["ffn_w_up"]),
        "ffn_w_down": f(inp["ffn_w_down"]),
        "lnbc": f(np.broadcast_to(np.stack([np.asarray(inp["d_ln_g"][0]), np.asarray(inp["d_ln_b"][0])])[None],
                                  (128, 2, 512))),
        "bs": f(np.asarray(inp["d_b_s"][0])[None]),
        "fgbc": f(np.broadcast_to(np.asarray(inp["final_norm_g"])[None], (128, D))),
    }
    maps = []
    for b in cores:
        m = dict(shared)
        m["x"] = f(inp["x"][b])
        m["pos"] = np.ascontiguousarray(np.broadcast_to(np.asarray(inp["positions"][b], dtype=np.int32)[None], (128, S)))
        m["cols"] = make_cols(b, inp)
        maps.append(m)
    return maps


def kernel(**inputs):
    nc = get_program(None)
    maps = make_in_maps(inputs, list(range(8)))
    res = run_bass_kernel_spmd(nc, maps, core_ids=list(range(8)))
    return np.stack([np.asarray(r["out"], dtype=np.float32) for r in res.results], axis=0)
```

```python
import numpy as np
from contextlib import ExitStack
import concourse.bass as bass
import concourse.mybir as mybir
from concourse.bass_utils import run_bass_kernel_spmd

F32 = mybir.dt.float32
BF16 = mybir.dt.bfloat16
I32 = mybir.dt.int32
AF = mybir.ActivationFunctionType
ALU = mybir.AluOpType

S = 2048
D = 1024
NT = 4
TT = 512
DFF = 2816
NPAIR = 22
EPS = 1e-6
SLOT_ELEMS = 4096
NSLOT = 4
PREFETCH = 2
JG = 4

FFN_TILES = [(0, 512, 0, 512, 0), (510, 512, 512, 510, 2), (1020, 512, 1022, 510, 2),
             (1530, 512, 1532, 510, 2), (2040, 8, 2042, 6, 2)]


class Res:
    __slots__ = ("name", "w", "r", "dsem", "dcnt", "tile_id")

    def __init__(self, name):
        self.name = name
        self.w = {}
        self.r = {}
        self.dsem = None
        self.dcnt = 0
        self.tile_id = None


class Ctx:
    def __init__(self, nc, es):
        self.nc = nc
        self.es = es
        self.eng = dict(pe=nc.tensor, act=nc.scalar, dve=nc.vector, pool=nc.gpsimd, sp=nc.sync)
        self.sem = {k: es.enter_context(nc.semaphore("s_" + k)) for k in self.eng}
        self.cnt = {k: 0 for k in self.eng}
        self.waited = {k: {} for k in self.eng}
        self.nsem = 0
        self.uid = 0

    def name(self, base):
        self.uid += 1
        return "%s_%d" % (base, self.uid)

    def newsem(self, base="d"):
        self.nsem += 1
        return self.es.enter_context(self.nc.semaphore("%s%d" % (base, self.nsem)))

    def _deps(self, e, reads, writes, skip=None):
        need = {}
        own_pe = id(self.sem["pe"]) if e == "pe" else skip

        def add(d):
            for k, (s, v) in d.items():
                if k == own_pe:
                    continue
                if k not in need or need[k][1] < v:
                    need[k] = (s, v)

        for r in reads:
            add(r.w)
        for w in writes:
            add(w.w)
            add(w.r)
        wd = self.waited[e]
        for k, (s, v) in need.items():
            if wd.get(k, 0) < v:
                self.eng[e].wait_ge(s, v)
                wd[k] = v

    def _post(self, t, reads, writes):
        s, v = t
        k = id(s)
        for r in reads:
            if k not in r.r or r.r[k][1] < v:
                r.r[k] = t
        for w in writes:
            w.w[k] = t
            w.r = {}

    def op(self, e, fn, reads=(), writes=(), inc=True):
        self._deps(e, reads, writes)
        ins = fn()
        if inc:
            self.cnt[e] += 1
            ins.then_inc(self.sem[e], 1)
            t = (self.sem[e], self.cnt[e])
        else:
            t = (self.sem[e], self.cnt[e] + 1)
        self._post(t, reads, writes)
        return t

    def dma(self, q, out, in_, reads=(), writes=(), done=None):
        d = done if done is not None else writes[0]
        if d.dsem is None:
            d.dsem = self.newsem()
        self._deps(q, reads, writes, skip=id(d.dsem))
        d.dcnt += 16
        self.eng[q].dma_start(out=out, in_=in_).then_inc(d.dsem, 16)
        t = (d.dsem, d.dcnt)
        self._post(t, reads, writes)
        return t

    def sync_to(self, e, engs=("pe", "act", "dve")):
        for f in engs:
            k = id(self.sem[f])
            if self.waited[e].get(k, 0) < self.cnt[f]:
                self.eng[e].wait_ge(self.sem[f], self.cnt[f])
                self.waited[e][k] = self.cnt[f]

    def barrier(self, engs=("pe", "act", "dve")):
        for e in engs:
            for f in engs:
                k = id(self.sem[f])
                if self.waited[e].get(k, 0) < self.cnt[f]:
                    self.eng[e].wait_ge(self.sem[f], self.cnt[f])
                    self.waited[e][k] = self.cnt[f]


class Ring:
    def __init__(self, items):
        self.items = items
        self.i = 0

    def next(self):
        it = self.items[self.i % len(self.items)]
        self.i += 1
        return it


class WStream:
    def __init__(self, c, T, slots, slot_res, plan):
        self.c = c
        self.T = T
        self.slots = slots
        self.res = slot_res
        self.plan = plan
        self.rec = []
        self.k = 0
        self.issued = 0

    def _issue(self, idx, fn):
        s = idx % NSLOT
        res = self.res[s]
        pieces = fn(self.T, self.slots[s])
        for (o, i) in pieces:
            self.c.dma("pool", o, i, reads=(), writes=(res,))
        res.tile_id = idx

    def get(self, fn):
        k = self.k
        self.k += 1
        self.rec.append(fn)
        if self.plan is None:
            upto = k
            fns = self.rec
        else:
            upto = min(k + PREFETCH, len(self.plan) - 1)
            fns = self.plan
        while self.issued <= upto:
            self._issue(self.issued, fns[self.issued])
            self.issued += 1
        s = k % NSLOT
        return self.slots[s], self.res[s], k

    def check(self, res, k):
        assert res.tile_id == k, "weight slot reused before consumption (%s vs %s)" % (res.tile_id, k)


def wview(slot, nk, ncols):
    return slot[:, 0:nk * ncols].rearrange("p (k n) -> p k n", k=nk)


def build(plan=None, stop=None):
    nc = bass.Bass("TRN2", target_bir_lowering=False)
    es = ExitStack()
    c = Ctx(nc, es)
    stages = ["load", "l0mix", "l0ffn", "l1mix", "l1ffn"]
    upto = len(stages) if stop is None else stages.index(stop)

    def din(name, shape, dt=F32):
        return nc.dram_tensor(name, list(shape), dt, kind="ExternalInput").ap()

    T = {}
    T["x"] = din("x", [S, D])
    T["pos"] = din("pos", [128, S], I32)
    T["cols"] = din("cols", [128, NCOLS])
    T["lnbc"] = din("lnbc", [128, 2, 512])
    T["bs"] = din("bs", [1, 4, 128])
    T["fgbc"] = din("fgbc", [128, D])
    T["ada_w"] = din("ada_w", [2, D, 6 * D])
    T["ab_w_in"] = din("ab_w_in", [D, 2048])
    T["b_mix_w"] = din("b_mix_w", [4, 128, 128])
    T["ab_w_out"] = din("ab_w_out", [D, D])
    T["cd_w_in"] = din("cd_w_in", [D, 1440])
    T["c_w_uq"] = din("c_w_uq", [256, 768])
    T["c_w_ukv"] = din("c_w_ukv", [128, 1024])
    T["d_w_sT"] = din("d_w_sT", [4, 128, 128])
    T["cd_w_out"] = din("cd_w_out", [D, D])
    T["ffn_w_up"] = din("ffn_w_up", [2, D, 2 * DFF])
    T["ffn_w_down"] = din("ffn_w_down", [2, DFF, D])
    out_d = nc.dram_tensor("out", [S, D], F32, kind="ExternalOutput").ap()

    def sb(shape, dt, base="t", stack=None):
        return (stack or es).enter_context(nc.sbuf_tensor(c.name(base), list(shape), dt))

    xT = sb([128, 8, S], F32, "xT")
    x_res = [[Res("x%d_%d" % (ci, t)) for t in range(NT)] for ci in range(8)]
    cols = sb([128, NCOLS], F32, "cols")
    cols_res = Res("cols")
    modc = sb([128, 2, 48], F32, "modc")
    modA = sb([128, 2, 2, 8], F32, "modA")
    mod_res_l = [Res("mod0"), Res("mod1")]
    cact = sb([128, 8], BF16, "cact")
    cact_res = Res("cact")
    bg = []
    ident = sb([128, 128], F32, "ident")
    ones_bf = sb([128, 128], BF16, "ones")
    ones_f = sb([1, 128], F32, "onesf")
    setup_res = Res("setup")
    slots = [sb([128, SLOT_ELEMS], BF16, "slot") for _ in range(NSLOT)]
    slot_res = [Res("slot%d" % i) for i in range(NSLOT)]
    ps = es.enter_context(nc.psum_tensor("ps", [128, 8, 512], F32))
    bank_res = [Res("bank%d" % i) for i in range(8)]
    ws = WStream(c, T, slots, slot_res, plan)

    def bank(i):
        return ps[:, i, :]

    def colv(name, j=0, n=1):
        o = COLOFF[name] + j
        return cols[:, o:o + n]

    c.dma("sp", cols[:], T["cols"], writes=(cols_res,))
    c.op("pool", lambda: nc.gpsimd.memset(ident[:], 1.0), writes=(setup_res,))
    c.op("pool", lambda: nc.gpsimd.affine_select(out=ident[:], in_=ident[:], pattern=[[-1, 128]],
                                                  compare_op=ALU.is_equal, fill=0.0, base=0,
                                                  channel_multiplier=1),
         reads=(setup_res,), writes=(setup_res,))
    c.op("pool", lambda: nc.gpsimd.memset(ones_bf[:], 1.0), writes=(setup_res,))
    c.op("pool", lambda: nc.gpsimd.memset(ones_f[:], 1.0), writes=(setup_res,))
    tril = sb([128, 128], F32, "tril")
    c.op("pool", lambda: nc.gpsimd.memset(tril[:], 1.0), writes=(setup_res,))
    c.op("pool", lambda: nc.gpsimd.affine_select(out=tril[:], in_=tril[:], pattern=[[1, 128]],
                                                  compare_op=ALU.is_ge, fill=0.0, base=0,
                                                  channel_multiplier=-1),
         reads=(setup_res,), writes=(setup_res,))
    invc = sb([128, 4, 16], F32, "invc")
    invc_res = Res("invc")
    i16 = sb([128, 16], F32, "i16")
    c.op("pool", lambda: nc.gpsimd.iota(i16[:], pattern=[[1, 16]], base=1, channel_multiplier=0,
                                        allow_small_or_imprecise_dtypes=True), writes=(invc_res,))
    for g in range(4):
        c.op("dve", lambda g=g: nc.vector.tensor_scalar(out=invc[:, g, :], in0=i16[:], scalar1=float(2 ** (g + 1)),
                                                        scalar2=None, op0=ALU.min),
             reads=(invc_res,), writes=(invc_res,))
    c.op("dve", lambda: nc.vector.reciprocal(out=invc[:], in_=invc[:]), reads=(invc_res,), writes=(invc_res,))

    tab_d = nc.dram_tensor("ropetab", [2, 128, S], F32, kind="Internal").ap()
    tabw_res = Res("tabw")
    c.op("act", lambda: nc.scalar.activation(out=cact[:], in_=colv("c", 0, 8), func=AF.Silu),
         reads=(cols_res,), writes=(cact_res,))
    with ExitStack() as ph:
        stg = [sb([128, D], F32, "xstg", ph) for _ in range(3)]
        stg_res = [Res("xstg%d" % i) for i in range(3)]
        ring = Ring(list(range(8)))
        if upto >= 3:
            Ct = sb([128, S], F32, "Ct0", ph)
            Sg = sb([128, S], F32, "Sg0", ph)
            tab_res = Res("tab0")
            posi = sb([128, S], I32, "posi", ph)
            ang = sb([128, S], F32, "ang", ph)
            kq = sb([128, S], F32, "kq", ph)
            rr = sb([128, S], F32, "rr", ph)
            pr, ar, kr, rres = Res("posi"), Res("ang"), Res("kq"), Res("rr")
            c.dma("sp", posi[:], T["pos"], writes=(pr,))
            c.op("dve", lambda: nc.vector.tensor_copy(out=ang[:], in_=posi[:]), reads=(pr,), writes=(ar,))
            c.op("dve", lambda: nc.vector.tensor_scalar(out=ang[:], in0=ang[:], scalar1=colv("inv_freq"), scalar2=None,
                                                        op0=ALU.mult), reads=(ar, cols_res), writes=(ar,))
            MAGIC = 12582912.0
            C1 = 6.28125
            C2 = 2.0 * np.pi - 6.28125
            for which, dst in ((0, Sg), (1, Ct)):
                src = ang
                if which == 1:
                    c.op("dve", lambda: nc.vector.tensor_scalar(out=rr[:], in0=ang[:], scalar1=float(np.pi / 2), scalar2=None,
                                                                op0=ALU.add), reads=(ar,), writes=(rres,))
                    src = rr
                sres = ar if which == 0 else rres
                c.op("dve", lambda src=src: nc.vector.tensor_scalar(out=kq[:], in0=src[:], scalar1=float(1.0 / (2 * np.pi)),
                                                                    scalar2=MAGIC, op0=ALU.mult, op1=ALU.add),
                     reads=(sres,), writes=(kr,))
                c.op("dve", lambda: nc.vector.tensor_scalar(out=kq[:], in0=kq[:], scalar1=-MAGIC, scalar2=None, op0=ALU.add),
                     reads=(kr,), writes=(kr,))
                c.op("dve", lambda src=src: nc.vector.scalar_tensor_tensor(out=rr[:], in0=kq[:], scalar=-C1, in1=src[:],
                                                                           op0=ALU.mult, op1=ALU.add),
                     reads=(kr, sres, rres), writes=(rres,))
                c.op("dve", lambda: nc.vector.scalar_tensor_tensor(out=rr[:], in0=kq[:], scalar=float(-C2), in1=rr[:],
                                                                   op0=ALU.mult, op1=ALU.add),
                     reads=(kr, rres), writes=(rres,))
                c.op("dve", lambda: nc.vector.tensor_scalar(out=rr[:], in0=rr[:], scalar1=float(np.pi), scalar2=float(-np.pi),
                                                            op0=ALU.min, op1=ALU.max), reads=(rres,), writes=(rres,))
                c.op("act", lambda dst=dst: nc.scalar.activation(out=dst[:], in_=rr[:], func=AF.Sin),
                     reads=(rres,), writes=(tab_res,))
            c.op("dve", lambda: nc.vector.tensor_scalar(out=Sg[:], in0=Sg[:], scalar1=colv("rope_sign"), scalar2=None,
                                                        op0=ALU.mult), reads=(tab_res, cols_res), writes=(tab_res,))

        for tb in range(16):
            si = tb % 3
            c.dma("sp", stg[si][:], T["x"][tb * 128:(tb + 1) * 128, :], writes=(stg_res[si],))
            tt = tb // 4
            for half in range(2):
                b = ring.next()
                for q in range(4):
                    ci = half * 4 + q
                    c.op("pe", lambda ci=ci, b=b, q=q, si=si: nc.tensor.transpose(
                        ps[:, b, q * 128:(q + 1) * 128], stg[si][:, ci * 128:(ci + 1) * 128], ident[:]),
                        reads=(stg_res[si], setup_res), writes=(bank_res[b],), inc=(q == 3))
                eng = "act"
                dst = xT[:, half * 4:half * 4 + 4, tb * 128:(tb + 1) * 128]
                src = ps[:, b, :].rearrange("p (q n) -> p q n", q=4)
                wr = tuple(x_res[half * 4 + q][tt] for q in range(4))
                if eng == "act":
                    c.op("act", lambda dst=dst, src=src: nc.scalar.activation(out=dst, in_=src, func=AF.Copy),
                         reads=(bank_res[b],), writes=wr)
                else:
                    c.op("dve", lambda dst=dst, src=src: nc.vector.tensor_copy(out=dst, in_=src),
                         reads=(bank_res[b],), writes=wr)
        if upto >= 3:
            c.dma("sp", tab_d[0], Ct[:], reads=(tab_res,), writes=(), done=tabw_res)
            c.dma("sp", tab_d[1], Sg[:], reads=(tab_res,), writes=(), done=tabw_res)
            nc.tensor.wait_ge(tabw_res.dsem, tabw_res.dcnt)
            nc.vector.wait_ge(tabw_res.dsem, tabw_res.dcnt)
            nc.scalar.wait_ge(tabw_res.dsem, tabw_res.dcnt)
        c.barrier()


    def mod_tile(l, i, b):
        slot, sres, k = ws.get(lambda T, slot, i=i, l=l: [(
            wview(slot, 8, 512),
            T["ada_w"][l].rearrange("(k p) n -> p k n", p=128)[:, :, i * 512:(i + 1) * 512])])
        wv = wview(slot, 8, 512)
        for jj in range(4):
            for kk in range(8):
                ws.check(sres, k)
                c.op("pe", lambda jj=jj, kk=kk: nc.tensor.matmul(
                    ps[:, b, jj:jj + 1], lhsT=wv[:, kk, jj * 128:(jj + 1) * 128], rhs=cact[:, kk:kk + 1],
                    start=(kk == 0), stop=(kk == 7)),
                    reads=(sres, cact_res), writes=(bank_res[b],), inc=(kk == 7))
        c.op("dve", lambda: nc.vector.tensor_tensor(out=modc[:, l, i * 4:i * 4 + 4], in0=ps[:, b, 0:4],
                                                    in1=colv("ada_b", l * 48 + i * 4, 4), op=ALU.add),
             reads=(bank_res[b], cols_res), writes=(mod_res_l[l],))

    def mod_A(l, n):
        gname, so = (("norm1_g", 8), ("norm2_g", 32))[n]
        c.op("dve", lambda: nc.vector.scalar_tensor_tensor(
            out=modA[:, l, n, :], in0=modc[:, l, so:so + 8], scalar=1.0, in1=colv(gname, l * 8, 8),
            op0=ALU.add, op1=ALU.mult),
            reads=(mod_res_l[l], cols_res), writes=(mod_res_l[l],))

    def bg_step(ring):
        if bg:
            bg.pop(0)(ring.next())

    def bg_flush(ring):
        while bg:
            bg.pop(0)(ring.next())

    def emit_norm(h, h_res, Acol, Bcol, ph_parent, l, small=False):
        with ExitStack() as ph:
            nsq = 2 if small else 4
            sq = [sb([128, S], BF16, "sq", ph) for _ in range(nsq)]
            sq_res = [Res("sq%d" % i) for i in range(nsq)]
            rstd = sb([128, S], F32, "rstd", ph)
            rstd_res = Res("rstd")
            if small:
                tmp = [sb([128, S], F32, "ntmp", ph)] * 2
                tmp_res = [Res("ntmp0")] * 2
            else:
                tmp = [sb([128, S], F32, "ntmp", ph) for _ in range(2)]
                tmp_res = [Res("ntmp0"), Res("ntmp1")]
            banks = [0, 1, 2, 3]
            for ci in range(8):
                si = ci % nsq
                if ci % 2 == 0:
                    c.op("act", lambda ci=ci, si=si: nc.scalar.activation(out=sq[si][:], in_=xT[:, ci, :], func=AF.Square),
                         reads=tuple(x_res[ci]), writes=(sq_res[si],))
                else:
                    c.op("dve", lambda ci=ci, si=si: nc.vector.tensor_tensor(out=sq[si][:], in0=xT[:, ci, :], in1=xT[:, ci, :], op=ALU.mult),
                         reads=tuple(x_res[ci]), writes=(sq_res[si],))
                for t in range(NT):
                    c.op("pe", lambda ci=ci, t=t, si=si: nc.tensor.matmul(
                        ps[:, banks[t], :], lhsT=ones_bf[:], rhs=sq[si][:, t * TT:(t + 1) * TT],
                        start=(ci == 0), stop=(ci == 7)),
                        reads=(sq_res[si], setup_res), writes=(bank_res[banks[t]],), inc=True)
            for t in range(NT):
                c.op("act", lambda t=t: nc.scalar.activation(out=rstd[:, t * TT:(t + 1) * TT], in_=ps[:, banks[t], :],
                                                           func=AF.Ln, bias=colv("eps"), scale=1.0 / D),
                     reads=(bank_res[banks[t]], cols_res), writes=(rstd_res,))
            c.op("act", lambda: nc.scalar.activation(out=rstd[:], in_=rstd[:], func=AF.Exp, scale=-0.5),
                 reads=(rstd_res,), writes=(rstd_res,))
            for ci in range(8):
                c.op("dve", lambda ci=ci: nc.vector.tensor_tensor(out=tmp[ci % 2][:], in0=xT[:, ci, :], in1=rstd[:],
                                                                  op=ALU.mult),
                     reads=tuple(x_res[ci]) + (rstd_res,), writes=(tmp_res[ci % 2],))
                c.op("act", lambda ci=ci: nc.scalar.activation(out=h[:, ci, :], in_=tmp[ci % 2][:], func=AF.Identity,
                                                             bias=Bcol(ci), scale=Acol(ci)),
                     reads=(tmp_res[ci % 2], mod_res_l[l], cols_res), writes=(h_res,))
            c.barrier()

    def proj_tile(wv, sres, k, col0, ncol, rhs_fn, rhs_res, nk, b, t0=None, n=TT, tok=None):
        for kk in range(nk):
            ws.check(sres, k)
            c.op("pe", lambda kk=kk: nc.tensor.matmul(
                ps[0:ncol, b, 0:n], lhsT=wv[:, kk, col0:col0 + ncol], rhs=rhs_fn(kk),
                start=(kk == 0), stop=(kk == nk - 1)),
                reads=(sres,) + tuple(rhs_res), writes=(bank_res[b],), inc=(kk == nk - 1))

    def emit_l0_mixer():
        l = 0
        with ExitStack() as ph:
            h = sb([128, 8, S], BF16, "h", ph)
            h_res = Res("h")
            y = sb([128, 8, S], BF16, "y", ph)
            y_res = Res("y")
            emit_norm(h, h_res, lambda ci: modA[:, l, 0, ci:ci + 1], lambda ci: modc[:, l, ci:ci + 1], ph, l)
            ring = Ring(list(range(8)))

            def wtile(i):
                return ws.get(lambda T, slot, i=i: [(
                    wview(slot, 8, 512),
                    T["ab_w_in"].rearrange("(k p) n -> p k n", p=128)[:, :, i * 512:(i + 1) * 512])])

            with ExitStack() as pa:
                cg = [sb([128, S], F32, "cg", pa) for _ in range(2)]
                cg_res = [Res("cg0"), Res("cg1")]
                u = [sb([128, S + 2], F32, "u", pa) for _ in range(2)]
                u_res = [Res("u0"), Res("u1")]
                cv = sb([128, S], F32, "cv", pa)
                cv_res = Res("cv")
                for i in range(2):
                    c.op("dve", lambda i=i: nc.vector.memset(u[i][:, 0:2], 0.0), writes=(u_res[i],))
                for i in range(4):
                    p2 = i % 2
                    s3, r3, k3 = ws.get(lambda T, slot, i=i: [
                        (wview(slot, 8, 384)[:, :, q * 128:(q + 1) * 128],
                         T["ab_w_in"].rearrange("(k p) n -> p k n", p=128)[:, :, sq_ * 512 + i * 128:sq_ * 512 + (i + 1) * 128])
                        for q, sq_ in enumerate((1, 2, 0))])
                    w3 = wview(s3, 8, 384)
                    for t in range(NT):
                        b = ring.next()
                        proj_tile(w3, r3, k3, 0, 128, lambda kk, t=t: h[:, kk, t * TT:(t + 1) * TT], (h_res,), 8, b)
                        c.op("act", lambda b=b, t=t, p2=p2: nc.scalar.activation(
                            out=cg[p2][:, t * TT:(t + 1) * TT], in_=ps[:, b, :], func=AF.Copy),
                            reads=(bank_res[b],), writes=(cg_res[p2],))
                    for t in range(NT):
                        b = ring.next()
                        proj_tile(w3, r3, k3, 128, 128, lambda kk, t=t: h[:, kk, t * TT:(t + 1) * TT], (h_res,), 8, b)
                        c.op("dve", lambda b=b, t=t, p2=p2: nc.vector.tensor_tensor(
                            out=u[p2][:, 2 + t * TT:2 + (t + 1) * TT], in0=ps[:, b, :], in1=cg[p2][:, t * TT:(t + 1) * TT],
                            op=ALU.mult),
                            reads=(bank_res[b], cg_res[p2]), writes=(u_res[p2],))
                    c.op("dve", lambda i=i, p2=p2: nc.vector.tensor_scalar(
                        out=cv[:], in0=u[p2][:, 2:S + 2], scalar1=colv("a_conv", i * 3 + 2), scalar2=None, op0=ALU.mult),
                        reads=(u_res[p2], cols_res), writes=(cv_res,))
                    for kq in (1, 0):
                        c.op("dve", lambda i=i, p2=p2, kq=kq: nc.vector.scalar_tensor_tensor(
                            out=cv[:], in0=u[p2][:, kq:S + kq], scalar=colv("a_conv", i * 3 + kq), in1=cv[:],
                            op0=ALU.mult, op1=ALU.add),
                            reads=(u_res[p2], cols_res, cv_res), writes=(cv_res,))
                    for t in range(NT):
                        b = ring.next()
                        proj_tile(w3, r3, k3, 256, 128, lambda kk, t=t: h[:, kk, t * TT:(t + 1) * TT], (h_res,), 8, b)
                        c.op("dve", lambda b=b, t=t, i=i: nc.vector.tensor_tensor(
                            out=y[:, i, t * TT:(t + 1) * TT], in0=ps[:, b, :], in1=cv[:, t * TT:(t + 1) * TT], op=ALU.mult),
                            reads=(bank_res[b], cv_res), writes=(y_res,))
                    bg_step(ring)
                    bg_step(ring)
                bg_flush(ring)
                c.barrier()
            with ExitStack() as pb:
                PADW = 16
                pbuf = [sb([128, S + PADW], F32, "pbuf", pb) for _ in range(3)]
                pb_res = [Res("pb0"), Res("pb1"), Res("pb2")]
                pl = [sb([128, S], BF16, "pl", pb) for _ in range(2)]
                pl_res = [Res("pl0"), Res("pl1")]
                t16 = sb([128, 16], F32, "t16", pb)
                t16_res = Res("t16")
                for i in range(3):
                    c.op("dve", lambda i=i: nc.vector.memset(pbuf[i][:, 0:PADW], 0.0), writes=(pb_res[i],))
                sp_, rp, kp = wtile(3)
                wp = wview(sp_, 8, 512)
                sm, rm, km = ws.get(lambda T, slot: [(wview(slot, 4, 128), T["b_mix_w"].rearrange("g c d -> c g d"))])
                wm = wview(sm, 4, 128)
                for g in range(4):
                    for t in range(NT):
                        b = ring.next()
                        proj_tile(wp, rp, kp, g * 128, 128, lambda kk, t=t: h[:, kk, t * TT:(t + 1) * TT], (h_res,), 8, b)
                        c.op("act", lambda b=b, t=t: nc.scalar.activation(
                            out=pbuf[0][:, PADW + t * TT:PADW + (t + 1) * TT], in_=ps[:, b, :], func=AF.Copy),
                            reads=(bank_res[b],), writes=(pb_res[0],))
                    src = 0
                    for lv in range(g + 1):
                        dst = 1 if src != 1 else 2
                        sh = 2 ** lv
                        c.op("dve", lambda src=src, dst=dst, sh=sh: nc.vector.tensor_tensor(
                            out=pbuf[dst][:, PADW:PADW + S], in0=pbuf[src][:, PADW:PADW + S],
                            in1=pbuf[src][:, PADW - sh:PADW + S - sh], op=ALU.add),
                            reads=(pb_res[src],), writes=(pb_res[dst],))
                        src = dst
                    w = float(2 ** (g + 1))
                    c.op("dve", lambda src=src, g=g, w=w: nc.vector.scalar_tensor_tensor(
                        out=pl[g % 2][:], in0=pbuf[src][:, PADW:PADW + S], scalar=1.0 / w, in1=pbuf[0][:, PADW:PADW + S],
                        op0=ALU.mult, op1=ALU.subtract),
                        reads=(pb_res[src], pb_res[0]), writes=(pl_res[g % 2],))
                    c.op("dve", lambda src=src, g=g: nc.vector.tensor_tensor(
                        out=t16[:], in0=pbuf[src][:, PADW:PADW + 16], in1=invc[:, g, :], op=ALU.mult),
                        reads=(pb_res[src], invc_res), writes=(t16_res,))
                    c.op("dve", lambda g=g: nc.vector.tensor_tensor(
                        out=pl[g % 2][:, 0:16], in0=t16[:], in1=pbuf[0][:, PADW:PADW + 16], op=ALU.subtract),
                        reads=(t16_res, pb_res[0], pl_res[g % 2]), writes=(pl_res[g % 2],))
                    for t in range(NT):
                        b = ring.next()
                        ws.check(rm, km)
                        c.op("pe", lambda b=b, t=t, g=g: nc.tensor.matmul(
                            ps[:, b, :], lhsT=wm[:, g, :], rhs=pl[g % 2][:, t * TT:(t + 1) * TT], start=True, stop=True),
                            reads=(rm, pl_res[g % 2]), writes=(bank_res[b],))
                        c.op("act", lambda b=b, t=t, g=g: nc.scalar.activation(
                            out=y[:, 4 + g, t * TT:(t + 1) * TT], in_=ps[:, b, :], func=AF.Copy, scale=colv("b_scale", g)),
                            reads=(bank_res[b], cols_res), writes=(y_res,))
                c.barrier()
            emit_out_proj("ab_w_out", lambda kk, t: y[:, kk, t * TT:(t + 1) * TT], (y_res,), l, ring)
            c.barrier()

    def emit_out_proj(wname, rhs_fn, rhs_res, l, ring):
        for half in range(2):
            so, ro, ko = ws.get(lambda T, slot, half=half: [(
                wview(slot, 8, 512),
                T[wname].rearrange("(k p) n -> p k n", p=128)[:, :, half * 512:(half + 1) * 512])])
            wo = wview(so, 8, 512)
            for dq in range(4):
                dc = half * 4 + dq
                for t in range(NT):
                    b = ring.next()
                    proj_tile(wo, ro, ko, dq * 128, 128, lambda kk, t=t: rhs_fn(kk, t), rhs_res, 8, b)
                    c.op("dve", lambda b=b, t=t, dc=dc: nc.vector.scalar_tensor_tensor(
                        out=xT[:, dc, t * TT:(t + 1) * TT], in0=ps[:, b, :], scalar=modc[:, l, 16 + dc:17 + dc],
                        in1=xT[:, dc, t * TT:(t + 1) * TT], op0=ALU.mult, op1=ALU.add),
                        reads=(bank_res[b], mod_res_l[l], x_res[dc][t]), writes=(x_res[dc][t],))

    def emit_l1_mixer():
        l = 1
        cdw = lambda T: T["cd_w_in"].rearrange("(k p) n -> p k n", p=128)
        with ExitStack() as ph0:
            yd = sb([128, 4, S], BF16, "yd", ph0)
            yd_res = Res("yd")
            mod_A(1, 0)
            with ExitStack() as ph:
                h = sb([128, 8, S], BF16, "h", ph)
                h_res = Res("h")
                emit_norm(h, h_res, lambda ci: modA[:, l, 0, ci:ci + 1], lambda ci: modc[:, l, ci:ci + 1], ph, l)
                u = sb([128, 4, S], BF16, "u", ph)
                u_res = Res("u")
                vn = sb([128, 16, 512], BF16, "vn", ph)
                vn_res = Res("vn")
                lnbc = sb([128, 2, 512], F32, "lnbc", ph)
                wsf = sb([128, 4, 128], F32, "wsf", ph)
                wsT = sb([128, 4, 128], BF16, "wsT", ph)
                bsr = sb([1, 4, 128], F32, "bsr", ph)
                ld_res = Res("l1ld")
                wsT_res = Res("wsT")
                vg = [sb([128, 512], F32, "vg", ph) for _ in range(3)]
                vg_res = [Res("vg0"), Res("vg1"), Res("vg2")]
                st6 = sb([128, 3, 6], F32, "st6", ph)
                mv = sb([128, 3, 2], F32, "mv", ph)
                st_res = [Res("st0"), Res("st1"), Res("st2")]
                c.sync_to("sp")
                c.dma("sp", lnbc[:], T["lnbc"], writes=(ld_res,))
                c.dma("sp", wsf[:], T["d_w_sT"].rearrange("g s t -> s g t"), writes=(ld_res,))
                c.dma("sp", bsr[:], T["bs"], writes=(ld_res,))
                for g in range(4):
                    c.op("dve", lambda g=g: nc.vector.tensor_tensor(out=wsT[:, g, :], in0=wsf[:, g, :], in1=tril[:], op=ALU.mult),
                         reads=(ld_res, setup_res), writes=(wsT_res,))
                ring = Ring([0, 1, 2, 3, 4, 5])
                su, ru, ku = ws.get(lambda T, slot: [(wview(slot, 8, 512), cdw(T)[:, :, 416:928])])
                wu = wview(su, 8, 512)
                for i in range(4):
                    for t in range(NT):
                        b = ring.next()
                        proj_tile(wu, ru, ku, i * 128, 128, lambda kk, t=t: h[:, kk, t * TT:(t + 1) * TT], (h_res,), 8, b)
                        c.op("act", lambda b=b, i=i, t=t: nc.scalar.activation(
                            out=u[:, i, t * TT:(t + 1) * TT], in_=ps[:, b, :], func=AF.Gelu_apprx_tanh),
                            reads=(bank_res[b],), writes=(u_res,))
                sv, rv, kv_ = ws.get(lambda T, slot: [(wview(slot, 8, 512), cdw(T)[:, :, 928:1440])])
                wv_ = wview(sv, 8, 512)
                mhalf = sb([128, 1], F32, "mhalf", ph)
                mh_res = Res("mhalf")
                c.op("dve", lambda: nc.vector.memset(mhalf[:], -0.5), writes=(mh_res,))
                ve = sb([128, 3, 1], F32, "ve", ph)
                rsd = sb([128, 3, 1], F32, "rsd", ph)
                ve_res = [Res("ve%d" % i) for i in range(3)]
                rsd_res = [Res("rsd%d" % i) for i in range(3)]

                def v_stats(tb):
                    b = ring.next()
                    p3 = tb % 3
                    for kk in range(8):
                        ws.check(rv, kv_)
                        c.op("pe", lambda kk=kk: nc.tensor.matmul(
                            ps[:, b, :], lhsT=h[:, kk, tb * 128:(tb + 1) * 128], rhs=wv_[:, kk, :],
                            start=(kk == 0), stop=(kk == 7)),
                            reads=(rv, h_res), writes=(bank_res[b],), inc=(kk == 7))
                    c.op("act", lambda: nc.scalar.activation(out=vg[p3][:], in_=ps[:, b, :], func=AF.Gelu_apprx_tanh),
                         reads=(bank_res[b],), writes=(vg_res[p3],))
                    c.op("dve", lambda: nc.vector.bn_stats(out=st6[:, p3, :], in_=vg[p3][:]),
                         reads=(vg_res[p3],), writes=(st_res[p3],))
                    c.op("dve", lambda: nc.vector.bn_aggr(out=mv[:, p3, :], in_=st6[:, p3, :]),
                         reads=(st_res[p3],), writes=(st_res[p3],))
                    c.op("dve", lambda: nc.vector.tensor_scalar(out=ve[:, p3, :], in0=mv[:, p3, 1:2], scalar1=EPS, scalar2=None,
                                                                op0=ALU.add),
                         reads=(st_res[p3],), writes=(ve_res[p3],))
                    c.op("pool", lambda: nc.gpsimd.tensor_tensor(out=rsd[:, p3, :], in0=ve[:, p3, :], in1=mhalf[:], op=ALU.pow),
                         reads=(ve_res[p3], mh_res), writes=(rsd_res[p3],))

                def v_norm(tb):
                    p3 = tb % 3
                    c.op("dve", lambda: nc.vector.tensor_scalar(
                        out=vg[p3][:], in0=vg[p3][:], scalar1=mv[:, p3, 0:1], scalar2=rsd[:, p3, :],
                        op0=ALU.subtract, op1=ALU.mult),
                        reads=(vg_res[p3], st_res[p3], rsd_res[p3]), writes=(vg_res[p3],))
                    c.op("dve", lambda: nc.vector.tensor_tensor(out=vg[p3][:], in0=vg[p3][:], in1=lnbc[:, 0, :], op=ALU.mult),
                         reads=(vg_res[p3], ld_res), writes=(vg_res[p3],))
                    c.op("dve", lambda: nc.vector.tensor_tensor(out=vn[:, tb, :], in0=vg[p3][:], in1=lnbc[:, 1, :], op=ALU.add),
                         reads=(vg_res[p3], ld_res), writes=(vn_res,))

                v_stats(0)
                for tb in range(1, 16):
                    v_stats(tb)
                    v_norm(tb - 1)
                v_norm(15)
                bsbc = sb([128, 4, TT], F32, "bsbc", ph)
                bsbc_res = Res("bsbc")
                for g in range(4):
                    b = ring.next()
                    for q in range(4):
                        c.op("pe", lambda g=g, q=q, b=b: nc.tensor.matmul(
                            ps[:, b, q * 128:(q + 1) * 128], lhsT=ones_f[0:1, :], rhs=bsr[0:1, g, :], start=True, stop=True),
                            reads=(ld_res, setup_res), writes=(bank_res[b],), inc=(q == 3))
                    c.op("act", lambda g=g, b=b: nc.scalar.activation(out=bsbc[:, g, :], in_=ps[:, b, :], func=AF.Copy),
                         reads=(bank_res[b],), writes=(bsbc_res,))
                mt = [sb([128, TT], F32, "mt", ph) for _ in range(2)]
                mt_res = [Res("mt0"), Res("mt1")]
                for g in range(4):
                    for t in range(NT):
                        b = ring.next()
                        for q in range(4):
                            n = t * 4 + q
                            c.op("pe", lambda g=g, n=n, q=q, b=b: nc.tensor.matmul(
                                ps[:, b, q * 128:(q + 1) * 128], lhsT=vn[:, n, g * 128:(g + 1) * 128], rhs=wsT[:, g, :],
                                start=True, stop=True),
                                reads=(vn_res, wsT_res), writes=(bank_res[b],), inc=(q == 3))
                        mi = (g * NT + t) % 2
                        c.op("dve", lambda g=g, b=b, mi=mi: nc.vector.tensor_tensor(
                            out=mt[mi][:], in0=ps[:, b, :], in1=bsbc[:, g, :], op=ALU.add),
                            reads=(bank_res[b], bsbc_res), writes=(mt_res[mi],))
                        c.op("dve", lambda g=g, t=t, mi=mi: nc.vector.tensor_tensor(
                            out=yd[:, g, t * TT:(t + 1) * TT], in0=mt[mi][:], in1=u[:, g, t * TT:(t + 1) * TT], op=ALU.mult),
                            reads=(mt_res[mi], u_res), writes=(yd_res,))
                c.barrier()
            Ct = sb([128, S], F32, "Ct", ph0)
            Sg = sb([128, S], F32, "Sg", ph0)
            tab_res = Res("tab")
            qn = sb([128, 2, S], BF16, "qn", ph0)
            qn_res = Res("qn")
            kvn = sb([128, S], BF16, "kvn", ph0)
            kvn_res = Res("kvn")
            kpe = sb([128, S], BF16, "kpe", ph0)
            kpe_res = Res("kpe")
            c.sync_to("sp")
            nc.sync.wait_ge(tabw_res.dsem, tabw_res.dcnt)
            c.dma("sp", Ct[:], tab_d[0], writes=(tab_res,))
            c.dma("sp", Sg[:], tab_d[1], writes=(tab_res,))
            with ExitStack() as ph:
                h = sb([128, 8, S], BF16, "h", ph)
                h_res = Res("h")
                emit_norm(h, h_res, lambda ci: modA[:, l, 0, ci:ci + 1], lambda ci: modc[:, l, ci:ci + 1], ph, l, small=True)
                sqt = [sb([128, TT], BF16, "sqt", ph) for _ in range(2)]
                sqt_res = [Res("sqt0"), Res("sqt1")]
                rs = [sb([128, TT], F32, "rs", ph) for _ in range(2)]
                rs_res = [Res("rs0"), Res("rs1")]
                t12 = [sb([128, TT], F32, "t12", ph) for _ in range(2)]
                t12_res = [Res("t1"), Res("t2")]
                ring = Ring([0, 1, 2, 3, 4, 5, 6, 7])
                s1, r1, k1 = ws.get(lambda T, slot: [(wview(slot, 8, 416), cdw(T)[:, :, 0:416])])
                w1 = wview(s1, 8, 416)
                s2, r2, k2 = ws.get(lambda T, slot: [
                    (wview(slot, 8, 192)[:, :, 64:96], cdw(T)[:, :, 384:416]),
                    (wview(slot, 8, 192)[:, :, 96 + 64:96 + 80], cdw(T)[:, :, 400:416]),
                    (wview(slot, 8, 192)[:, :, 96 + 80:96 + 96], cdw(T)[:, :, 384:400])])
                w2 = wview(s2, 8, 192)
                hr = lambda t: (lambda kk: h[:, kk, t * TT:(t + 1) * TT])

                def rms_from_banks(banks, nfeat, gname, dsts, t, par):
                    bsum = ring.next()
                    for i, b in enumerate(banks):
                        c.op("act", lambda b=b, i=i: nc.scalar.activation(out=sqt[i % 2][:], in_=ps[:, b, :], func=AF.Square),
                             reads=(bank_res[b],), writes=(sqt_res[i % 2],))
                        c.op("pe", lambda i=i, bsum=bsum: nc.tensor.matmul(
                            ps[:, bsum, :], lhsT=ones_bf[:], rhs=sqt[i % 2][:], start=(i == 0), stop=(i == len(banks) - 1)),
                            reads=(sqt_res[i % 2], setup_res), writes=(bank_res[bsum],), inc=True)
                    c.op("act", lambda: nc.scalar.activation(out=rs[par][:], in_=ps[:, bsum, :], func=AF.Ln,
                                                           bias=colv("eps"), scale=1.0 / nfeat),
                         reads=(bank_res[bsum], cols_res), writes=(rs_res[par],))
                    c.op("act", lambda: nc.scalar.activation(out=rs[par][:], in_=rs[par][:], func=AF.Exp, scale=-0.5),
                         reads=(rs_res[par],), writes=(rs_res[par],))
                    for i, b in enumerate(banks):
                        dst, dres = dsts[i]
                        c.op("dve", lambda b=b, i=i, dst=dst: nc.vector.scalar_tensor_tensor(
                            out=dst, in0=ps[:, b, :], scalar=colv(gname, i), in1=rs[par][:], op0=ALU.mult, op1=ALU.mult),
                            reads=(bank_res[b], cols_res, rs_res[par]), writes=(dres,))

                for t in range(NT):
                    sl = slice(t * TT, (t + 1) * TT)
                    bq = [ring.next(), ring.next()]
                    for cc in range(2):
                        proj_tile(w1, r1, k1, cc * 128, 128, hr(t), (h_res,), 8, bq[cc])
                    rms_from_banks(bq, 256.0, "q_norm_g", [(qn[:, 0, sl], qn_res), (qn[:, 1, sl], qn_res)], t, 0)
                    bk = ring.next()
                    proj_tile(w1, r1, k1, 256, 128, hr(t), (h_res,), 8, bk)
                    rms_from_banks([bk], 128.0, "kv_norm_g", [(kvn[:, sl], kvn_res)], t, 1)
                    b1, b2 = ring.next(), ring.next()
                    proj_tile(w2, r2, k2, 0, 96, hr(t), (h_res,), 8, b1)
                    proj_tile(w2, r2, k2, 96, 96, hr(t), (h_res,), 8, b2)
                    c.op("dve", lambda: nc.vector.tensor_tensor(out=t12[0][64:96, :], in0=ps[64:96, b1, :], in1=Ct[64:96, sl], op=ALU.mult),
                         reads=(bank_res[b1], tab_res), writes=(t12_res[0],))
                    c.op("dve", lambda: nc.vector.tensor_tensor(out=t12[1][64:96, :], in0=ps[64:96, b2, :], in1=Sg[64:96, sl], op=ALU.mult),
                         reads=(bank_res[b2], tab_res), writes=(t12_res[1],))
                    c.op("dve", lambda: nc.vector.tensor_tensor(out=kpe[64:96, sl], in0=t12[0][64:96, :], in1=t12[1][64:96, :], op=ALU.add),
                         reads=(t12_res[0], t12_res[1]), writes=(kpe_res,))
                c.barrier()
            with ExitStack() as ph:
                yc = sb([128, 4, S], BF16, "yc", ph)
                yc_res = Res("yc")
                QT = [sb([128, S], BF16, "QT", ph) for _ in range(2)]
                KT = [sb([128, S], BF16, "KT", ph) for _ in range(2)]
                QT_res = [Res("QT0"), Res("QT1")]
                KT_res = [Res("KT0"), Res("KT1")]
                QTp_res = [Res("QTp0"), Res("QTp1")]
                KTp_res = [Res("KTp0"), Res("KTp1")]
                Va = [sb([128, 16, 128], BF16, "Va", ph) for _ in range(2)]
                Va_res = [Res("Va0"), Res("Va1")]
                pt = [sb([128, TT], BF16, "pt", ph) for _ in range(5)]
                pt_res = [Res("pt%d" % i) for i in range(5)]
                t12 = [sb([128, TT], F32, "t12", ph) for _ in range(2)]
                t12_res = [Res("t1"), Res("t2")]
                rt = sb([128, TT], F32, "rt", ph)
                rt_res = Res("rt")
                c.op("dve", lambda: nc.vector.memset(Va[0][:, :, 64:128], 1.0), writes=(Va_res[0],))
                c.op("dve", lambda: nc.vector.memset(Va[1][:, :, 0:64], 1.0), writes=(Va_res[1],))
                sq_, rq, kq_ = ws.get(lambda T, slot: [
                    (wview(slot, 2, 768), T["c_w_uq"].rearrange("(k p) n -> p k n", p=128))
                ] + [
                    (slot[:, 1536 + kk * 768:1536 + (kk + 1) * 768].rearrange("p (h e) -> p h e", h=8)[:, :, d0:d0 + 16],
                     T["c_w_uq"][kk * 128:(kk + 1) * 128, :].rearrange("p (h e) -> p h e", e=96)[:, :, s0:s0 + 16])
                    for kk in range(2) for (d0, s0) in ((64, 80), (80, 64))])
                wqa = wview(sq_, 2, 768)
                wqb = sq_[:, 1536:3072].rearrange("p (k n) -> p k n", k=2)
                skv, rkv, kkv = ws.get(lambda T, slot: [(slot[:, 0:1024], T["c_w_ukv"])])
                wkv = skv[:, 0:1024]
                sring = Ring([0, 1, 2, 3])
                pring = Ring([4, 5])
                aring = Ring([6, 7])
                ptring = Ring([0, 1, 2, 3, 4])
                scale = float(96.0 ** -0.5)

                def prep_pieces(hd):
                    p2 = hd % 2
                    pieces = []

                    def piece_t(t):
                        sl = slice(t * TT, (t + 1) * TT)
                        ba, bb = pring.next(), pring.next()
                        for (wq_, bnk) in ((wqa, ba), (wqb, bb)):
                            for kk in range(2):
                                ws.check(rq, kq_)
                                c.op("pe", lambda kk=kk, wq_=wq_, bnk=bnk: nc.tensor.matmul(
                                    ps[0:96, bnk, :], lhsT=wq_[:, kk, hd * 96:(hd + 1) * 96], rhs=qn[:, kk, sl],
                                    start=(kk == 0), stop=(kk == 1)),
                                    reads=(rq, qn_res), writes=(bank_res[bnk],), inc=(kk == 1))
                        c.op("dve", lambda: nc.vector.tensor_copy(out=QT[p2][0:64, sl], in_=ps[0:64, ba, :]),
                             reads=(bank_res[ba],), writes=(QT_res[p2],))
                        c.op("dve", lambda: nc.vector.tensor_tensor(out=t12[0][64:96, :], in0=ps[64:96, ba, :], in1=Ct[64:96, sl], op=ALU.mult),
                             reads=(bank_res[ba], tab_res), writes=(t12_res[0],))
                        c.op("dve", lambda: nc.vector.tensor_tensor(out=t12[1][64:96, :], in0=ps[64:96, bb, :], in1=Sg[64:96, sl], op=ALU.mult),
                             reads=(bank_res[bb], tab_res), writes=(t12_res[1],))
                        c.op("dve", lambda: nc.vector.tensor_tensor(out=QT[p2][64:96, sl], in0=t12[0][64:96, :], in1=t12[1][64:96, :], op=ALU.add),
                             reads=(t12_res[0], t12_res[1]), writes=(QTp_res[p2],))
                    def piece_k(t):
                        sl = slice(t * TT, (t + 1) * TT)
                        bk = pring.next()
                        ws.check(rkv, kkv)
                        c.op("pe", lambda: nc.tensor.matmul(ps[0:64, bk, :], lhsT=wkv[:, hd * 128:hd * 128 + 64], rhs=kvn[:, sl],
                                                            start=True, stop=True),
                             reads=(rkv, kvn_res), writes=(bank_res[bk],))
                        c.op("dve", lambda: nc.vector.tensor_copy(out=KT[p2][0:64, sl], in_=ps[0:64, bk, :]),
                             reads=(bank_res[bk],), writes=(KT_res[p2],))

                    def piece_kpe():
                        c.op("dve", lambda: nc.vector.tensor_copy(out=KT[p2][64:96, :], in_=kpe[64:96, :]),
                             reads=(kpe_res,), writes=(KTp_res[p2],))

                    voff = 0 if p2 == 0 else 64

                    def piece_v(half):
                        bv = pring.next()
                        for q in range(8):
                            kb = half * 8 + q
                            ws.check(rkv, kkv)
                            c.op("pe", lambda kb=kb, q=q: nc.tensor.matmul(
                                ps[:, bv, q * 64:(q + 1) * 64], lhsT=kvn[:, kb * 128:(kb + 1) * 128],
                                rhs=wkv[:, hd * 128 + 64:hd * 128 + 128], start=True, stop=True),
                                reads=(rkv, kvn_res), writes=(bank_res[bv],), inc=(q == 7))
                        c.op("dve", lambda: nc.vector.tensor_copy(
                            out=Va[p2][:, half * 8:half * 8 + 8, voff:voff + 64],
                            in_=ps[:, bv, :].rearrange("p (q e) -> p q e", q=8)),
                            reads=(bank_res[bv],), writes=(Va_res[p2],))

                    for t in range(NT):
                        pieces.append(lambda t=t: piece_t(t))
                        pieces.append(lambda t=t: piece_k(t))
                    pieces.append(piece_kpe)
                    pieces.append(lambda: piece_v(0))
                    pieces.append(lambda: piece_v(1))
                    return pieces

                LOOK = 3

                def attn(hd, pieces):
                    p2 = hd % 2
                    steps = []
                    for qt in range(NT):
                        nkb = 4 * qt + 4
                        for kb in range(nkb):
                            if kb < 4 * qt:
                                q0, n = qt * TT, TT
                            else:
                                q0 = kb * 128
                                n = (qt + 1) * TT - q0
                            steps.append((qt, kb, nkb, q0, n))
                    state = {}
                    pending = []

                    def score(i):
                        qt, kb, nkb, q0, n = steps[i]
                        sb_ = sring.next()
                        c.op("pe", lambda: nc.tensor.matmul(
                            ps[:, sb_, 0:n], lhsT=KT[p2][0:96, kb * 128:(kb + 1) * 128], rhs=QT[p2][0:96, q0:q0 + n],
                            start=True, stop=True),
                            reads=(KT_res[p2], QT_res[p2], KTp_res[p2], QTp_res[p2]), writes=(bank_res[sb_],))
                        pi_ = ptring.next()
                        c.op("act", lambda: nc.scalar.activation(out=pt[pi_][:, 0:n], in_=ps[:, sb_, 0:n], func=AF.Exp, scale=scale),
                             reads=(bank_res[sb_],), writes=(pt_res[pi_],))
                        if kb >= 4 * qt:
                            c.op("dve", lambda: nc.vector.tensor_tensor(out=pt[pi_][:, 0:128], in0=pt[pi_][:, 0:128], in1=tril[:], op=ALU.mult),
                                 reads=(pt_res[pi_], setup_res), writes=(pt_res[pi_],))
                        state[i] = pi_

                    def pv(i):
                        qt, kb, nkb, q0, n = steps[i]
                        if kb == 0:
                            state["ab"] = aring.next()
                        ab = state["ab"]
                        pi_ = state.pop(i)
                        o0 = q0 - qt * TT
                        c.op("pe", lambda: nc.tensor.matmul(
                            ps[:, ab, o0:o0 + n], lhsT=Va[p2][:, kb, :], rhs=pt[pi_][:, 0:n],
                            start=(kb == 0), stop=(kb == nkb - 1)),
                            reads=(Va_res[p2], pt_res[pi_]), writes=(bank_res[ab],))
                        if kb == nkb - 1:
                            pending.append((i + 4, lambda: normalize(qt, ab)))

                    def normalize(qt, ab):
                        if True:
                            sl = slice(qt * TT, (qt + 1) * TT)
                            if p2 == 0:
                                c.op("act", lambda: nc.scalar.activation(out=rt[0:64, :], in_=ps[64:128, ab, :], func=AF.Ln),
                                     reads=(bank_res[ab],), writes=(rt_res,))
                                c.op("act", lambda: nc.scalar.activation(out=rt[0:64, :], in_=rt[0:64, :], func=AF.Exp, scale=-1.0),
                                     reads=(rt_res,), writes=(rt_res,))
                                c.op("dve", lambda: nc.vector.tensor_tensor(out=yc[0:64, hd // 2, sl], in0=ps[0:64, ab, :], in1=rt[0:64, :], op=ALU.mult),
                                     reads=(bank_res[ab], rt_res), writes=(yc_res,))
                            else:
                                c.op("act", lambda: nc.scalar.activation(out=rt[64:128, :], in_=ps[0:64, ab, :], func=AF.Ln),
                                     reads=(bank_res[ab],), writes=(rt_res,))
                                c.op("act", lambda: nc.scalar.activation(out=rt[64:128, :], in_=rt[64:128, :], func=AF.Exp, scale=-1.0),
                                     reads=(rt_res,), writes=(rt_res,))
                                c.op("dve", lambda: nc.vector.tensor_tensor(out=yc[64:128, hd // 2, sl], in0=ps[64:128, ab, :], in1=rt[64:128, :], op=ALU.mult),
                                     reads=(bank_res[ab], rt_res), writes=(yc_res,))

                    ns = len(steps)
                    every = max(1, ns // (len(pieces) + 1)) if pieces else ns + 1
                    for i in range(min(LOOK, ns)):
                        score(i)
                    for i in range(ns):
                        if i + LOOK < ns:
                            score(i + LOOK)
                        pv(i)
                        while pending and pending[0][0] <= i:
                            pending.pop(0)[1]()
                        if pieces and (i + 1) % every == 0:
                            pieces.pop(0)()
                    while pending:
                        pending.pop(0)[1]()
                    while pieces:
                        pieces.pop(0)()

                for pc in prep_pieces(0):
                    pc()
                for hd in range(8):
                    attn(hd, prep_pieces(hd + 1) if hd + 1 < 8 else [])
                c.barrier()
                ring = Ring(list(range(8)))
                emit_out_proj("cd_w_out", lambda kk, t: (yc[:, kk, t * TT:(t + 1) * TT] if kk < 4 else yd[:, kk - 4, t * TT:(t + 1) * TT]),
                              (yc_res, yd_res), l, ring)
                c.barrier()

    def emit_ffn(l):
        with ExitStack() as ph:
            h = sb([128, 8, S], BF16, "h", ph)
            h_res = Res("h")
            mod_A(l, 1)
            if l == 0 and upto >= 3:
                for i in range(12):
                    bg.append(lambda b, i=i: mod_tile(1, i, b))
            emit_norm(h, h_res, lambda ci: modA[:, l, 1, ci:ci + 1], lambda ci: modc[:, l, 24 + ci:25 + ci], ph, l)
            acc = [[sb([128, S], F32, "acc", ph) for _ in range(2)] for _ in range(2)]
            acc_res = [[[Res("acc%d%d_%d" % (p, q, ti)) for ti in range(len(FFN_TILES))] for q in range(2)] for p in range(2)]
            act = [sb([128, JG, S], BF16, "act", ph) for _ in range(2)]
            act_res = [Res("act0"), Res("act1")]
            zring = Ring([0, 1, 2, 3, 4])
            dring = Ring([5, 6, 7])
            groups = [list(range(g0, min(g0 + JG, NPAIR))) for g0 in range(0, NPAIR, JG)]

            def emit_down(gi):
                grp = groups[gi]
                ng = len(grp)
                j0 = grp[0]
                sd, rd, kd = ws.get(lambda T, slot, j0=j0, ng=ng: [(
                    wview(slot, ng, 1024),
                    T["ffn_w_down"][l][j0 * 128:(j0 + ng) * 128, :].rearrange("(j p) d -> p j d", p=128))])
                wd = wview(sd, ng, 1024)
                a = act[gi % 2]
                ar = act_res[gi % 2]
                for dc in range(8):
                    for t in range(NT):
                        b = dring.next()
                        for jj in range(ng):
                            ws.check(rd, kd)
                            c.op("pe", lambda jj=jj, dc=dc, t=t, b=b: nc.tensor.matmul(
                                ps[:, b, :], lhsT=wd[:, jj, dc * 128:(dc + 1) * 128], rhs=a[:, jj, t * TT:(t + 1) * TT],
                                start=(jj == 0), stop=(jj == ng - 1)),
                                reads=(rd, ar), writes=(bank_res[b],), inc=(jj == ng - 1))
                        c.op("dve", lambda b=b, t=t, dc=dc: nc.vector.scalar_tensor_tensor(
                            out=xT[:, dc, t * TT:(t + 1) * TT], in0=ps[:, b, :], scalar=modc[:, l, 40 + dc:41 + dc],
                            in1=xT[:, dc, t * TT:(t + 1) * TT], op0=ALU.mult, op1=ALU.add),
                            reads=(bank_res[b], mod_res_l[l], x_res[dc][t]), writes=(x_res[dc][t],))

            for gi, grp in enumerate(groups):
                for jj, j in enumerate(grp):
                    par = j % 2
                    su, ru, ku = ws.get(lambda T, slot, j=j: [
                        (wview(slot, 8, 256)[:, :, 0:128],
                         T["ffn_w_up"][l].rearrange("(k p) n -> p k n", p=128)[:, :, j * 128:(j + 1) * 128]),
                        (wview(slot, 8, 256)[:, :, 128:256],
                         T["ffn_w_up"][l].rearrange("(k p) n -> p k n", p=128)[:, :, DFF + j * 128:DFF + (j + 1) * 128])])
                    wu = wview(su, 8, 256)
                    for br in range(2):
                        A = acc[par][br]
                        cw = COLOFF["ffn_conv"] + ((l * 2 + br) * NPAIR + j) * 3
                        for ti, (t0, n, o0, m, off) in enumerate(FFN_TILES):
                            Ar = acc_res[par][br][ti]
                            b = zring.next()
                            proj_tile(wu, ru, ku, br * 128, 128, lambda kk, t0=t0, n=n: h[:, kk, t0:t0 + n], (h_res,), 8, b, n=n)
                            c.op("act", lambda b=b, o0=o0, m=m, off=off, A=A, cw=cw: nc.scalar.activation(
                                out=A[:, o0:o0 + m], in_=ps[:, b, off:off + m], func=AF.Copy, scale=cols[:, cw + 2:cw + 3]),
                                reads=(bank_res[b], cols_res), writes=(Ar,))
                            for sh in (1, 2):
                                if off == 0:
                                    oo, mm_, so = o0 + sh, m - sh, 0
                                else:
                                    oo, mm_, so = o0, m, off - sh
                                c.op("dve", lambda b=b, oo=oo, mm_=mm_, so=so, A=A, cw=cw, sh=sh: nc.vector.scalar_tensor_tensor(
                                    out=A[:, oo:oo + mm_], in0=ps[:, b, so:so + mm_], scalar=cols[:, cw + 2 - sh:cw + 3 - sh],
                                    in1=A[:, oo:oo + mm_], op0=ALU.mult, op1=ALU.add),
                                    reads=(bank_res[b], cols_res, Ar), writes=(Ar,))
                    c.op("act", lambda par=par: nc.scalar.activation(out=acc[par][0][:], in_=acc[par][0][:], func=AF.Silu),
                         reads=tuple(acc_res[par][0]), writes=tuple(acc_res[par][0]))
                    c.op("dve", lambda par=par, gi=gi, jj=jj: nc.vector.tensor_tensor(
                        out=act[gi % 2][:, jj, :], in0=acc[par][0][:], in1=acc[par][1][:], op=ALU.mult),
                        reads=tuple(acc_res[par][0]) + tuple(acc_res[par][1]), writes=(act_res[gi % 2],))
                    if jj == 0 and gi > 0:
                        emit_down(gi - 1)
                    bg_step(dring)
            emit_down(len(groups) - 1)
            bg_flush(dring)
            c.barrier()

    def emit_output(final_norm):
        with ExitStack() as ph:
            ostg = [sb([128, D], F32, "ostg", ph) for _ in range(2)]
            ostg_res = [Res("ostg0"), Res("ostg1")]
            out_res = Res("out")
            if final_norm:
                gbc = sb([128, D], F32, "gbc", ph)
                gbc_res = Res("gbc")
                c.sync_to("sp")
                c.dma("sp", gbc[:], T["fgbc"], writes=(gbc_res,))
                junk = [sb([128, 512], BF16, "junk", ph) for _ in range(2)]
                junk_res = [Res("junk0"), Res("junk1")]
                ss = sb([128, 2, 2], F32, "ss", ph)
                rs1 = sb([128, 2, 1], F32, "rs1", ph)
                ss_res = [Res("ss0"), Res("ss1")]
                mhalf = sb([128, 1], F32, "mhalf", ph)
                mh_res = Res("mhalf")
                c.op("dve", lambda: nc.vector.memset(mhalf[:], -0.5), writes=(mh_res,))
            ring = Ring([0, 1, 2, 3, 4, 5, 6, 7])
            for tb in range(16):
                tt = tb // 4
                si = tb % 2
                bs_ = []
                for half in range(2):
                    b = ring.next()
                    bs_.append(b)
                    for q in range(4):
                        ci = half * 4 + q
                        c.op("pe", lambda ci=ci, b=b, q=q, tb=tb: nc.tensor.transpose(
                            ps[:, b, q * 128:(q + 1) * 128], xT[:, ci, tb * 128:(tb + 1) * 128], ident[:]),
                            reads=(x_res[ci][tt], setup_res), writes=(bank_res[b],), inc=(q == 3))
                if final_norm:
                    for half in range(2):
                        b = bs_[half]
                        c.op("act", lambda half=half, b=b: nc.scalar.activation(
                            out=junk[half][:], in_=ps[:, b, :], func=AF.Square, accum_out=ss[:, si, half:half + 1]),
                            reads=(bank_res[b],), writes=(junk_res[half], ss_res[si]))
                    c.op("dve", lambda: nc.vector.tensor_tensor(out=rs1[:, si, :], in0=ss[:, si, 0:1], in1=ss[:, si, 1:2], op=ALU.add),
                         reads=(ss_res[si],), writes=(ss_res[si],))
                    c.op("dve", lambda: nc.vector.tensor_scalar(out=rs1[:, si, :], in0=rs1[:, si, :], scalar1=1.0 / D, scalar2=EPS,
                                                                op0=ALU.mult, op1=ALU.add),
                         reads=(ss_res[si],), writes=(ss_res[si],))
                    c.op("pool", lambda: nc.gpsimd.tensor_tensor(out=rs1[:, si, :], in0=rs1[:, si, :], in1=mhalf[:], op=ALU.pow),
                         reads=(ss_res[si], mh_res), writes=(ss_res[si],))
                    for half in range(2):
                        b = bs_[half]
                        c.op("dve", lambda half=half, b=b: nc.vector.scalar_tensor_tensor(
                            out=ostg[si][:, half * 512:(half + 1) * 512], in0=ps[:, b, :], scalar=rs1[:, si, :],
                            in1=gbc[:, half * 512:(half + 1) * 512], op0=ALU.mult, op1=ALU.mult),
                            reads=(bank_res[b], ss_res[si], gbc_res), writes=(ostg_res[si],))
                else:
                    for half in range(2):
                        b = bs_[half]
                        dst = ostg[si][:, half * 512:(half + 1) * 512]
                        if half == 0:
                            c.op("act", lambda dst=dst, b=b: nc.scalar.activation(out=dst, in_=ps[:, b, :], func=AF.Copy),
                                 reads=(bank_res[b],), writes=(ostg_res[si],))
                        else:
                            c.op("dve", lambda dst=dst, b=b: nc.vector.tensor_copy(out=dst, in_=ps[:, b, :]),
                                 reads=(bank_res[b],), writes=(ostg_res[si],))
                c.dma("sp", out_d[tb * 128:(tb + 1) * 128, :], ostg[si][:], reads=(ostg_res[si],), writes=(), done=out_res)
            nc.sync.wait_ge(out_res.dsem, out_res.dcnt)

    if upto >= 1:
        for i in range(4):
            mod_tile(0, i, i)
        mod_A(0, 0)
        for i in range(4, 12):
            bg.append(lambda b, i=i: mod_tile(0, i, b))
        emit_l0_mixer()
    if upto >= 2:
        emit_ffn(0)
    if upto >= 3:
        emit_l1_mixer()
    if upto >= 4:
        emit_ffn(1)
    emit_output(final_norm=(stop is None))
    es.close()
    return nc, ws.rec


COLSPEC = [("c", 8), ("ada_b", 96), ("norm1_g", 16), ("norm2_g", 16), ("final_g", 8), ("a_conv", 12),
           ("b_scale", 4), ("q_norm_g", 2), ("kv_norm_g", 1), ("ffn_conv", 2 * 2 * NPAIR * 3),
           ("inv_freq", 1), ("rope_sign", 1), ("eps", 1), ("pi", 1)]
COLOFF = {}
_o = 0
for _n, _w in COLSPEC:
    COLOFF[_n] = _o
    _o += _w
NCOLS = _o


def colify(v):
    v = np.asarray(v, dtype=np.float32)
    return np.ascontiguousarray(v.reshape(-1, 128).T)


def make_cols(b, inp):
    cols = np.zeros((128, NCOLS), np.float32)

    def put(name, arr):
        o = COLOFF[name]
        cols[:, o:o + arr.shape[1]] = arr

    put("c", colify(inp["c"][b]))
    put("ada_b", np.concatenate([colify(inp["ada_b"][l]) for l in range(2)], axis=1))
    put("norm1_g", np.concatenate([colify(inp["norm1_g"][l]) for l in range(2)], axis=1))
    put("norm2_g", np.concatenate([colify(inp["norm2_g"][l]) for l in range(2)], axis=1))
    put("final_g", colify(inp["final_norm_g"]))
    ac = np.asarray(inp["a_conv_w"][0], np.float32)
    put("a_conv", np.ascontiguousarray(ac.reshape(3, 4, 128).transpose(2, 1, 0).reshape(128, 12)))
    put("b_scale", colify(inp["b_scale"][0]))
    put("q_norm_g", colify(inp["c_q_norm_g"][0]))
    put("kv_norm_g", colify(inp["c_kv_norm_g"][0]))
    fc = np.asarray(inp["ffn_conv_w"], np.float32)
    fc = fc.reshape(2, 3, 2, NPAIR, 128).transpose(4, 0, 2, 3, 1).reshape(128, -1)
    put("ffn_conv", np.ascontiguousarray(fc))
    half = 16
    inv_freq = (np.float32(10000.0) ** (-(np.arange(half, dtype=np.float32) / np.float32(half)))).astype(np.float32)
    p = np.arange(128)
    put("inv_freq", inv_freq[p % 16][:, None])
    put("rope_sign", np.where((p % 32) < 16, -1.0, 1.0).astype(np.float32)[:, None])
    put("eps", np.full((128, 1), EPS, np.float32))
    put("pi", np.full((128, 1), np.pi, np.float32))
    return cols


_CACHE = {}


def get_program(stop=None):
    if stop not in _CACHE:
        _, rec = build(plan=None, stop=stop)
        nc, _ = build(plan=rec, stop=stop)
        _CACHE[stop] = nc
    return _CACHE[stop]


def make_in_maps(inp, cores):
    f = lambda a: np.ascontiguousarray(np.asarray(a, dtype=np.float32))
    shared = {
        "ada_w": f(inp["ada_w"]),
        "ab_w_in": f(inp["ab_w_in"][0]),
        "b_mix_w": f(inp["b_mix_w"][0]),
        "ab_w_out": f(inp["ab_w_out"][0]),
        "cd_w_in": f(inp["cd_w_in"][0]),
        "c_w_uq": f(inp["c_w_uq"][0]),
        "c_w_ukv": f(inp["c_w_ukv"][0]),
        "d_w_sT": f(np.transpose(np.asarray(inp["d_w_s"][0]), (0, 2, 1))),
        "cd_w_out": f(inp["cd_w_out"][0]),
        "ffn_w_up": f(inp["ffn_w_up"]),
        "ffn_w_down": f(inp["ffn_w_down"]),
        "lnbc": f(np.broadcast_to(np.stack([np.asarray(inp["d_ln_g"][0]), np.asarray(inp["d_ln_b"][0])])[None],
                                  (128, 2, 512))),
        "bs": f(np.asarray(inp["d_b_s"][0])[None]),
        "fgbc": f(np.broadcast_to(np.asarray(inp["final_norm_g"])[None], (128, D))),
    }
    maps = []
    for b in cores:
        m = dict(shared)
        m["x"] = f(inp["x"][b])
        m["pos"] = np.ascontiguousarray(np.broadcast_to(np.asarray(inp["positions"][b], dtype=np.int32)[None], (128, S)))
        m["cols"] = make_cols(b, inp)
        maps.append(m)
    return maps


def kernel(**inputs):
    nc = get_program(None)
    maps = make_in_maps(inputs, list(range(8)))
    res = run_bass_kernel_spmd(nc, maps, core_ids=list(range(8)))
    return np.stack([np.asarray(r["out"], dtype=np.float32) for r in res.results], axis=0)
```
